# Optimizing a Trainium2 kernel written in Bass

```python
import jax, jax.numpy as jnp
from jax import lax
import numpy as np

D_MODEL = 1024
BATCH = 4
SEQ = 4096
DEPTH = 2

PLE_DIM = 256
HEAD_DIM = 64
CONV_WIDTH = 256
CONV_HEADS = CONV_WIDTH // HEAD_DIM
LRU_WIDTH = 512
LRU_HEADS = LRU_WIDTH // HEAD_DIM
SGU_WIDTH = 256
SGU_HEADS = SGU_WIDTH // HEAD_DIM
MIX_WIDTH = CONV_WIDTH + LRU_WIDTH + SGU_WIDTH
IN_WIDTH = 2 * CONV_WIDTH + 2 * LRU_WIDTH + 2 * SGU_WIDTH
CONV_K = 31
LRU_CONV_K = 4
LRU_C = 8.0
CHUNK = 128
D_FF = 2816
FFN_CONV_K = 3
EPS = 1e-6
SPLITS = [CONV_WIDTH, 2 * CONV_WIDTH, 2 * CONV_WIDTH + LRU_WIDTH, 2 * CONV_WIDTH + 2 * LRU_WIDTH,
          2 * CONV_WIDTH + 2 * LRU_WIDTH + SGU_WIDTH]

kernel_name = "hybrid_conv_lru_sgu_encoder"


def rmsnorm(x, g):
    xf = x.astype(jnp.float32)
    y = xf * lax.rsqrt(jnp.mean(xf * xf, axis=-1, keepdims=True) + EPS)
    return (y * g.astype(jnp.float32)).astype(x.dtype)


def group_layernorm(x, g, b, n_groups):
    shp = x.shape
    xf = x.astype(jnp.float32).reshape(shp[:-1] + (n_groups, shp[-1] // n_groups))
    mu = jnp.mean(xf, axis=-1, keepdims=True)
    var = jnp.mean(jnp.square(xf - mu), axis=-1, keepdims=True)
    y = ((xf - mu) * lax.rsqrt(var + EPS)).reshape(shp)
    return (y * g.astype(jnp.float32) + b.astype(jnp.float32)).astype(x.dtype)


def dwconv(x, w, b, pad_l, pad_r):
    c = x.shape[-1]
    y = lax.conv_general_dilated(
        x, w[:, None, :].astype(x.dtype), window_strides=(1,), padding=[(pad_l, pad_r)],
        dimension_numbers=("NWC", "WIO", "NWC"), feature_group_count=c)
    return y + b.astype(x.dtype)


def conformer_conv(val, gate, w_dw, b_dw, gn_g, gn_b):
    h = val * jax.nn.sigmoid(gate)
    h = dwconv(h, w_dw, b_dw, CONV_K // 2, CONV_K // 2)
    h = group_layernorm(h, gn_g, gn_b, CONV_HEADS)
    return jax.nn.silu(h)


def _lin_combine(left, right):
    a1, b1 = left
    a2, b2 = right
    return a1 * a2, a2 * b1 + b2


def rglru_direction(x, w_conv, b_conv, w_a, b_a, w_x, b_x, lam):
    bsz, s, _ = x.shape
    xc = dwconv(x, w_conv, b_conv, LRU_CONV_K - 1, 0)
    xh = xc.reshape(bsz, s, LRU_HEADS, HEAD_DIM)
    r = jax.nn.sigmoid(jnp.einsum("bshd,hde->bshe", xh, w_a).reshape(bsz, s, LRU_WIDTH) + b_a)
    i = jax.nn.sigmoid(jnp.einsum("bshd,hde->bshe", xh, w_x).reshape(bsz, s, LRU_WIDTH) + b_x)
    log_a = LRU_C * r.astype(jnp.float32) * jax.nn.log_sigmoid(lam.astype(jnp.float32))
    a = jnp.exp(log_a)
    u = jnp.sqrt(-jnp.expm1(2.0 * log_a)) * (i * xc).astype(jnp.float32)
    _, h = lax.associative_scan(_lin_combine, (a, u), axis=1)
    return h.astype(x.dtype)


def chunked_sgu(u, v, ln_g, ln_b, w_s, b_s):
    bsz, s, _ = v.shape
    u = jax.nn.gelu(u)
    v = group_layernorm(jax.nn.gelu(v), ln_g, ln_b, 1)
    vh = v.reshape(bsz, s // CHUNK, CHUNK, SGU_HEADS, HEAD_DIM)
    mixed = jnp.einsum("hpq,bnqhd->bnphd", w_s, vh) + jnp.transpose(b_s)[None, None, :, :, None]
    return u * mixed.reshape(bsz, s, SGU_WIDTH)


def setup_inputs(seed: int = 0) -> dict:
    key = jax.random.key(seed)
    ks = iter(jax.random.split(key, 40))

    def nrm(shape, scale):
        return jax.random.normal(next(ks), shape, jnp.float32) * scale

    def gain(shape):
        return 1.0 + nrm(shape, 0.02)

    a_pow = jax.random.uniform(next(ks), (DEPTH, 2, LRU_WIDTH), jnp.float32, 0.9, 0.999)
    a_base = a_pow ** (1.0 / LRU_C)
    lam = jnp.log(a_base) - jnp.log1p(-a_base)
    return {
        "x": nrm((BATCH, SEQ, D_MODEL), 1.0),
        "p": nrm((DEPTH, BATCH, SEQ, PLE_DIM), 1.0),
        "norm_mix": gain((DEPTH, D_MODEL)),
        "w_in": nrm((DEPTH, D_MODEL, IN_WIDTH), D_MODEL ** -0.5),
        "conv_dw_w": nrm((DEPTH, CONV_K, CONV_WIDTH), CONV_K ** -0.5),
        "conv_dw_b": nrm((DEPTH, CONV_WIDTH), 0.02),
        "conv_gn_g": gain((DEPTH, CONV_WIDTH)),
        "conv_gn_b": nrm((DEPTH, CONV_WIDTH), 0.02),
        "lru_conv_w": nrm((DEPTH, 2, LRU_CONV_K, LRU_WIDTH), LRU_CONV_K ** -0.5),
        "lru_conv_b": nrm((DEPTH, 2, LRU_WIDTH), 0.02),
        "lru_wa": nrm((DEPTH, 2, LRU_HEADS, HEAD_DIM, HEAD_DIM), HEAD_DIM ** -0.5),
        "lru_ba": nrm((DEPTH, 2, LRU_WIDTH), 0.02),
        "lru_wx": nrm((DEPTH, 2, LRU_HEADS, HEAD_DIM, HEAD_DIM), HEAD_DIM ** -0.5),
        "lru_bx": nrm((DEPTH, 2, LRU_WIDTH), 0.02),
        "lru_lambda": lam,
        "sgu_ln_g": gain((DEPTH, SGU_WIDTH)),
        "sgu_ln_b": nrm((DEPTH, SGU_WIDTH), 0.02),
        "sgu_ws": nrm((DEPTH, SGU_HEADS, CHUNK, CHUNK), CHUNK ** -0.5),
        "sgu_bs": 1.0 + nrm((DEPTH, SGU_HEADS, CHUNK), 0.02),
        "out_norm": gain((DEPTH, MIX_WIDTH)),
        "w_out": nrm((DEPTH, MIX_WIDTH, D_MODEL), MIX_WIDTH ** -0.5),
        "norm_ffn": gain((DEPTH, D_MODEL)),
        "w_up": nrm((DEPTH, D_MODEL, 2 * D_FF), D_MODEL ** -0.5),
        "ffn_conv_w": nrm((DEPTH, FFN_CONV_K, 2 * D_FF), FFN_CONV_K ** -0.5),
        "ffn_conv_b": nrm((DEPTH, 2 * D_FF), 0.02),
        "w_down": nrm((DEPTH, D_FF, D_MODEL), D_FF ** -0.5),
        "norm_ple": gain((DEPTH, D_MODEL)),
        "w_ple_gate": nrm((DEPTH, D_MODEL, D_MODEL), D_MODEL ** -0.5),
        "b_ple_gate": nrm((DEPTH, D_MODEL), 0.02),
        "w_ple": nrm((DEPTH, PLE_DIM, D_MODEL), PLE_DIM ** -0.5),
        "ple_post_norm": gain((DEPTH, D_MODEL)),
        "final_norm": gain((D_MODEL,)),
    }


def reference(x, p, norm_mix, w_in, conv_dw_w, conv_dw_b, conv_gn_g, conv_gn_b,
              lru_conv_w, lru_conv_b, lru_wa, lru_ba, lru_wx, lru_bx, lru_lambda,
              sgu_ln_g, sgu_ln_b, sgu_ws, sgu_bs, out_norm, w_out, norm_ffn, w_up,
              ffn_conv_w, ffn_conv_b, w_down, norm_ple, w_ple_gate, b_ple_gate, w_ple,
              ple_post_norm, final_norm):
    for l in range(DEPTH):
        h = rmsnorm(x, norm_mix[l])
        z = h @ w_in[l]
        cv, cg, lx, lg, su, sv = jnp.split(z, SPLITS, axis=-1)

        ya = conformer_conv(cv, cg, conv_dw_w[l], conv_dw_b[l], conv_gn_g[l], conv_gn_b[l])

        h_fwd = rglru_direction(lx, lru_conv_w[l, 0], lru_conv_b[l, 0], lru_wa[l, 0], lru_ba[l, 0],
                                lru_wx[l, 0], lru_bx[l, 0], lru_lambda[l, 0])
        h_bwd = jnp.flip(rglru_direction(jnp.flip(lx, axis=1), lru_conv_w[l, 1], lru_conv_b[l, 1],
                                         lru_wa[l, 1], lru_ba[l, 1], lru_wx[l, 1], lru_bx[l, 1],
                                         lru_lambda[l, 1]), axis=1)
        yb = (h_fwd + h_bwd) * jax.nn.gelu(lg)

        yc = chunked_sgu(su, sv, sgu_ln_g[l], sgu_ln_b[l], sgu_ws[l], sgu_bs[l])

        g_out = out_norm[l]
        y = jnp.concatenate([
            rmsnorm(ya, g_out[:CONV_WIDTH]),
            rmsnorm(yb, g_out[CONV_WIDTH:CONV_WIDTH + LRU_WIDTH]),
            rmsnorm(yc, g_out[CONV_WIDTH + LRU_WIDTH:]),
        ], axis=-1)
        x = x + y @ w_out[l]

        h = rmsnorm(x, norm_ffn[l])
        f = dwconv(h @ w_up[l], ffn_conv_w[l], ffn_conv_b[l], FFN_CONV_K // 2, FFN_CONV_K // 2)
        fg, fu = jnp.split(f, 2, axis=-1)
        x = x + (jax.nn.gelu(fg) * fu) @ w_down[l]

        gate = jax.nn.sigmoid(rmsnorm(x, norm_ple[l]) @ w_ple_gate[l] + b_ple_gate[l])
        e = rmsnorm(p[l] @ w_ple[l], ple_post_norm[l])
        x = x + gate * e
    return rmsnorm(x, final_norm)
```

```python
import numpy as np
from contextlib import ExitStack
import concourse.bass as bass
import concourse.mybir as mybir

F32 = mybir.dt.float32
BF16 = mybir.dt.bfloat16
AF = mybir.ActivationFunctionType
ALU = mybir.AluOpType


SAME_ENG_GAP = 1000


class Ref:
    __slots__ = ("eng", "sem", "tick")

    def __init__(self, eng, sem=None, tick=None):
        self.eng, self.sem, self.tick = eng, sem, tick


def ap_range(ap):
    if isinstance(ap, tuple):
        return ap[1], 0, 1
    t = ap.tensor
    if type(t).__name__.startswith("DRam"):
        return None
    isz = mybir.dt.size(ap.dtype)
    dims = list(ap.ap)
    pstep = dims[0][0]
    off = ap.offset % pstep if pstep > 0 else ap.offset
    lo = hi = off
    for st, cnt in dims[1:]:
        if st >= 0:
            hi += st * (cnt - 1)
        else:
            lo += st * (cnt - 1)
    return t.name, lo * isz, (hi + 1) * isz


class Prog:
    NRING = 8

    def __init__(self, nc):
        self.nc = nc
        self.engs = {"pe": nc.tensor, "act": nc.scalar, "dve": nc.vector,
                     "pool": nc.gpsimd, "sp": nc.sync}
        self.ops = {e: [] for e in self.engs}
        self.count = {e: 0 for e in self.engs}
        self.waited = {e: {} for e in self.engs}
        self.acc = {}
        self.pe_pending = []
        self.ndma = {"sp": 0, "pool": 0, "act": 0}
        self.ncc = 0
        self.psum = []
        self.ps_i = 0

    def _collect(self, reads, writes):
        deps = []
        for ap in reads:
            r = ap_range(ap)
            if r is None:
                continue
            name, lo, hi = r
            for (l2, h2, ref, w) in self.acc.get(name, ()):
                if w and l2 < hi and lo < h2:
                    deps.append(ref)
        for ap in writes:
            r = ap_range(ap)
            if r is None:
                continue
            name, lo, hi = r
            for (l2, h2, ref, w) in self.acc.get(name, ()):
                if l2 < hi and lo < h2:
                    deps.append(ref)
        return deps

    def _update(self, reads, writes, ref):
        for ap in writes:
            r = ap_range(ap)
            if r is None:
                continue
            name, lo, hi = r
            lst = self.acc.setdefault(name, [])
            lst[:] = [e for e in lst if not (lo <= e[0] and e[1] <= hi)]
            lst.append((lo, hi, ref, True))
        for ap in reads:
            r = ap_range(ap)
            if r is None:
                continue
            name, lo, hi = r
            lst = self.acc.setdefault(name, [])
            if ref.eng != "dma":
                lst[:] = [e for e in lst if not (e[0] == lo and e[1] == hi and (not e[3]) and e[2].eng == ref.eng)]
            lst.append((lo, hi, ref, False))

    def _flush_pe(self):
        if not self.pe_pending:
            return
        ent = None
        for e_ in reversed(self.ops["pe"]):
            if e_[0] == "ins":
                ent = e_
                break
        if not ent[3]:
            ent[3] = True
            self.count["pe"] += 1
        for r in self.pe_pending:
            r.tick = self.count["pe"]
        self.pe_pending = []

    def _wait(self, eng, ref):
        if ref.eng == eng and eng == "pe":
            return
        if ref.eng == "pe" and ref.tick is None:
            self._flush_pe()
        if ref.eng == eng:
            if eng != "sp" and self.count[eng] - ref.tick >= SAME_ENG_GAP:
                return
        sem, val = ref.sem, ref.tick
        if self.waited[eng].get(sem, 0) >= val:
            return
        self.waited[eng][sem] = val
        self.ops[eng].append(["wait", sem, val])

    def op(self, eng, fn, reads=(), writes=(), flag=True):
        for ref in self._collect(reads, writes):
            self._wait(eng, ref)
        ref = Ref(eng, "S_" + eng)
        if eng == "pe":
            self.pe_pending.append(ref)
            self.ops[eng].append(["ins", fn, ref, False])
            if flag:
                self._flush_pe()
        else:
            self.count[eng] += 1
            ref.tick = self.count[eng]
            self.ops[eng].append(["ins", fn, ref, True])
        self._update(reads, writes, ref)
        return ref

    def dma(self, q, out, in_, reads=None, writes=None, **kw):
        nc = self.nc
        reads = [in_] if reads is None else reads
        writes = [out] if writes is None else writes
        for ref in self._collect(reads, writes):
            self._wait(q, ref)
        i = self.ndma[q]
        self.ndma[q] += 1
        ring = "D_%s_%d" % (q, i % self.NRING)
        val = 16 * (i // self.NRING + 1)
        if val > 16:
            self._wait(q, Ref("dma", ring, val - 16))
        ref = Ref("dma", ring, val)
        eng = self.engs[q]
        self.ops[q].append(["dma", (lambda: eng.dma_start(out=out, in_=in_, **kw)), ring])
        self._update(reads, writes, ref)
        return ref

    def collective(self, in_dram, out_dram, groups, reads, writes):
        nc = self.nc
        for ref in self._collect(reads, writes):
            self._wait("pool", ref)
        self.ncc += 1
        ref = Ref("cc", "S_cc", self.ncc)
        self.ops["pool"].append(["cc", (lambda: nc.gpsimd.collective_compute(
            "AllGather", ALU.bypass, replica_groups=groups,
            ins=[in_dram.ap().opt()], outs=[out_dram.ap().opt()])), "S_cc"])
        self._update(reads, writes, ref)
        return ref

    def wait_all(self, eng, refs):
        for r in refs:
            self._wait(eng, r)

    def emit(self, final_refs):
        nc = self.nc
        self._flush_pe()
        for r in final_refs:
            self._wait("sp", r)
        names = ["S_pe", "S_act", "S_dve", "S_pool", "S_sp", "S_cc"]
        for q in ("sp", "pool", "act"):
            names += ["D_%s_%d" % (q, i) for i in range(self.NRING)]
        with ExitStack() as st:
            sems = {n: st.enter_context(nc.semaphore(n)) for n in names}
            block = st.enter_context(nc.Block())
            prog = self

            def replay(eng_name, eng):
                for ent in prog.ops[eng_name]:
                    if ent[0] == "wait":
                        eng.wait_ge(sems[ent[1]], ent[2])
                    elif ent[0] == "ins":
                        ins = ent[1]()
                        if ent[3]:
                            ins.then_inc(sems["S_" + eng_name], 1)
                    elif ent[0] == "dma":
                        ent[1]().then_inc(sems[ent[2]], 16)
                    elif ent[0] == "cc":
                        ent[1]().then_inc(sems[ent[2]])

            @block.tensor
            def _(e):
                replay("pe", nc.tensor)

            @block.scalar
            def _(e):
                replay("act", nc.scalar)

            @block.vector
            def _(e):
                replay("dve", nc.vector)

            @block.gpsimd
            def _(e):
                replay("pool", nc.gpsimd)

            @block.sync
            def _(e):
                replay("sp", nc.sync)

from concourse.bass_utils import run_bass_kernel_spmd

D = 1024
SEQ = 4096
OWN = 2048
WL = [2304, 2176]
WMAX = 2304
NM = [2180, 2052]
NO = [2176, 2048]
D1S = [2183, 2055]
CPART = [SEQ - 1 - (d + 1) for d in D1S]
EPS = 1e-6
NCORES = 8
FILL_AC = 0
JUNK_RATIO = 0.0
FILL_B = 0

C_X = 0
C_H = C_X + 8 * WMAX
HW = WMAX + 2
C_PV = C_H + 8 * HW // 2
C_TMB = C_PV + 512
C_CONST = C_TMB + 1024
C_OVL = C_CONST + 512
ARENA = 53200

PVC = {}
_o = 0
for _n, _w in [("g_mix", 8), ("g_ffn", 8), ("g_ple", 8), ("g_post", 8), ("g_out", 8), ("b_gate", 8),
               ("g_final", 8), ("conv_w", 62), ("conv_b", 2), ("gn_g", 2), ("gn_b", 2),
               ("l_cw", 32), ("l_cb", 8), ("l_ba", 8), ("l_bx", 8), ("l_lam", 8),
               ("f_cw", 132), ("f_cb", 44), ("mask", 2), ("g_next", 8)]:
    PVC[_n] = _o
    _o += _w
NPV = _o
assert NPV <= 512


def blocks(a, b, step=512):
    if step == 512 and (b - a) % 512 != 0:
        nb = -(-(b - a) // 512)
        step = -(-(b - a) // nb)
        step = -(-step // 4) * 4
    out = []
    t = a
    while t < b:
        n = min(step, b - t)
        out.append((t, n))
        t += n
    return out


def build_program():
    nc = bass.Bass("TRN2", target_bir_lowering=False)
    dt = lambda name, shape, kind="ExternalInput": nc.dram_tensor(name, shape, F32, kind=kind)
    x_d = dt("x_fm", [8, 128, WMAX]).ap()
    p_d = dt("p_fm", [2, 2, 128, WMAX]).ap()
    pv_d = dt("pv", [2, 128, 512]).ap()
    tmb_d = dt("tmb", [2, 128, 1024]).ap()
    ident_d = dt("ident", [128, 128]).ap()
    win_d = dt("win_t", [2, 16, 128, 1024]).ap()
    wuv_d = dt("wuv_t", [2, 128, 4096]).ap()
    wout_d = dt("wout_t", [2, 128, 8192]).ap()
    wup_d = dt("wup_t", [2, 44, 128, 1024]).ap()
    wdn_d = dt("wdn_t", [2, 8, 128, 2816]).ap()
    wg_d = dt("wg_t", [2, 8, 128, 1024]).ap()
    wple_d = dt("wple_t", [2, 128, 2048]).ap()
    gates_d = dt("gates_t", [2, 128, 2048]).ap()
    wst_d = dt("wst_t", [2, 128, 512]).ap()
    y_d = dt("y_fm", [8, 128, OWN], kind="ExternalOutput").ap()
    cc_in = nc.dram_tensor("cc_in", [128, 4], F32)
    cc_out = nc.dram_tensor("cc_out", [256, 4], F32)

    st = ExitStack()
    AR = st.enter_context(nc.sbuf_tensor("arena", [128, ARENA], F32))
    PS = [st.enter_context(nc.psum_tensor("ps%d" % i, [128, 512], F32)) for i in range(8)]
    P = Prog(nc)
    state = {"ps": 0}

    def bank():
        b = PS[state["ps"] % (7 if (FILL_AC or FILL_B or JUNK_RATIO > 0) else 8)]
        state["ps"] += 1
        ents = P.acc.get(b.name, [])
        assert not (ents and ents[-1][3]), "PSUM bank %s reallocated before its contents were consumed" % b.name
        return b

    def f32v(c0, n):
        return AR[:, c0:c0 + n]

    def bf16v(c0, n):
        return AR[:, c0:c0 + (n + 1) // 2].bitcast(BF16)

    X = f32v(C_X, 8 * WMAX).rearrange("p (k t) -> p k t", k=8)
    H = bf16v(C_H, 8 * HW).rearrange("p (k t) -> p k t", k=8)
    PV = f32v(C_PV, 512)
    TMB = f32v(C_TMB, 1024)
    ONESB = bf16v(C_CONST, 128)
    IDENT = bf16v(C_CONST + 64, 128)
    BD64 = f32v(C_CONST + 128, 128)
    CO = f32v(C_CONST + 256, 4)
    CIN2 = f32v(C_CONST + 260, 8).rearrange("p (r c) -> p r c", r=2)
    CIN = f32v(C_CONST + 268, 4)
    CL = f32v(C_CONST + 272, 16)
    C1 = f32v(C_CONST + 340, 20)
    ST = f32v(C_CONST + 296, 32)
    CTMP = f32v(C_CONST + 328, 8)
    NEGH = f32v(C_CONST + 364, 1)
    HB = f32v(C_CONST + 368, 16)
    PH = f32v(C_CONST + 384, 4)
    CW2 = f32v(C_CONST + 392, 62)

    def pv(name, i=0):
        c = PVC[name] + i
        return PV[:, c:c + 1]

    def act(out, in_, func, bias=None, scale=None, accum=None):
        rd = [in_]
        kw = {}
        if bias is not None:
            kw["bias"] = bias
            if not isinstance(bias, float):
                rd.append(bias)
        if scale is not None:
            kw["scale"] = scale
            if not isinstance(scale, float):
                rd.append(scale)
        wr = [out]
        if accum is not None:
            kw["accum_out"] = accum
            wr.append(accum)
        P.op("act", lambda: nc.scalar.activation(out=out, in_=in_, func=func, **kw), reads=rd, writes=wr)

    def tt(out, a, b, op, eng="dve"):
        e = nc.vector if eng == "dve" else nc.gpsimd
        P.op(eng, lambda: e.tensor_tensor(out=out, in0=a, in1=b, op=op), reads=[a, b], writes=[out])

    def stt(out, in0, scalar, in1, op0, op1):
        rd = [in0, in1] + ([] if isinstance(scalar, float) else [scalar])
        P.op("dve", lambda: nc.vector.scalar_tensor_tensor(out=out, in0=in0, scalar=scalar, in1=in1, op0=op0, op1=op1),
             reads=rd, writes=[out])

    def ts(out, in0, s1, s2, op0, op1, eng="dve"):
        e = nc.vector if eng == "dve" else nc.gpsimd
        rd = [in0] + [s for s in (s1, s2) if not isinstance(s, float)]
        P.op(eng, lambda: e.tensor_scalar(out=out, in0=in0, scalar1=s1, scalar2=s2, op0=op0, op1=op1), reads=rd, writes=[out])

    TMBJ = TMB.bitcast(BF16)[:, 0:512]

    def mm(ps, lhsT, rhs, start, stop):
        P.op("pe", lambda: nc.tensor.matmul(ps, lhsT, rhs, start=start, stop=stop), reads=[lhsT, rhs], writes=[ps], flag=stop)
        if JUNK_RATIO > 0:
            state["grp"] = state.get("grp", 0) + 1
            if stop:
                state["jacc"] = state.get("jacc", 0.0) + state["grp"] * state.get("jr", 0.0)
                state["grp"] = 0
                while state["jacc"] >= 1.0:
                    state["jacc"] -= 1.0
                    P.op("pe", lambda: nc.tensor.matmul(PS[7][:, 0:512], ONESB, TMBJ, start=True, stop=True),
                         reads=[ONESB, TMBJ], writes=[PS[7][:, 0:512]], flag=False)

    def memset(ap, val, eng="pool"):
        e = nc.vector if eng == "dve" else nc.gpsimd
        P.op(eng, lambda: e.memset(ap, val), writes=[ap])

    def wload(dst_bf16, src_dram):
        P.dma("pool", dst_bf16, src_dram)

    def rsqrt(dst, src, scale):
        act(dst, src, AF.Ln, bias=EPS, scale=scale)
        act(dst, dst, AF.Exp, scale=-0.5)

    memset(H[:, :, 0:1], 0.0, eng="dve")
    memset(H[:, :, WMAX + 1:WMAX + 2], 0.0, eng="dve")
    memset(AR[:, C_OVL:C_OVL + 12000], 0.0, eng="dve")
    memset(AR[:, C_OVL + 12000:ARENA], 0.0, eng="pool")
    memset(ONESB, 1.0, eng="dve")
    memset(BD64, 0.0, eng="dve")
    memset(BD64[0:64, 0:64], 1.0, eng="dve")
    memset(BD64[64:128, 64:128], 1.0, eng="dve")
    wload(IDENT, ident_d)
    memset(NEGH, -0.5, eng="dve")

    NSQ = C_OVL + 20000

    def norm_H(gname, t_lo, t_hi):
        SQ = bf16v(NSQ, 8 * 512).rearrange("p (k t) -> p k t", k=8)
        RS = f32v(NSQ + 2048, 512)
        for (t0, n) in blocks(t_lo, t_hi):
            for k in range(8):
                act(SQ[:, k, 0:n], X[:, k, t0:t0 + n], AF.Square)
            ps = bank()
            for k in range(8):
                mm(ps[:, 0:n], ONESB, SQ[:, k, 0:n], k == 0, k == 7)
            rsqrt(RS[:, 0:n], ps[:, 0:n], 1.0 / D)
            for k in range(8):
                stt(H[:, k, 1 + t0:1 + t0 + n], X[:, k, t0:t0 + n], pv(gname, k), RS[:, 0:n], ALU.mult, ALU.mult)

    def wout_apply(WO, nk, Ytiles, W):
        for m in range(8):
            for (t0, n) in blocks(0, W):
                ps = bank()
                for k in range(nk):
                    mm(ps[:, 0:n], WO[:, k, m * 128:(m + 1) * 128], Ytiles[k][:, t0:t0 + n], k == 0, k == nk - 1)
                tt(X[:, m, t0:t0 + n], X[:, m, t0:t0 + n], ps[:, 0:n], ALU.add)

    def pe_filler(k):
        for _ in range(k):
            P.op("pe", lambda: nc.tensor.matmul(PS[7][:, 0:128], ONESB, IDENT, start=True, stop=True),
                 reads=[ONESB, IDENT], writes=[PS[7][:, 0:128]], flag=False)

    def run_threads(gens, width=None, fill=0):
        pending = list(gens)
        active = []
        width = width or len(pending)
        while pending or active:
            while pending and len(active) < width:
                active.append(pending.pop(0))
            for g in list(active):
                try:
                    next(g)
                except StopIteration:
                    active.remove(g)
            if fill:
                pe_filler(fill)

    finals = []
    for l in range(2):
        W = WL[l]
        NMl, NOl, D1, CP = NM[l], NO[l], D1S[l], CPART[l]
        ZA, ZB = NMl + 16, D1 + 5
        P.dma("sp", PV, pv_d[l])
        P.dma("sp", TMB, tmb_d[l])
        if l == 0:
            for bi_, (t0, n) in enumerate(blocks(0, WMAX)):
                P.dma("sp" if bi_ % 2 == 0 else "act", X[:, :, t0:t0 + n], x_d[:, :, t0:t0 + n].rearrange("k p t -> p k t"))
        act(CTMP, PV[:, PVC["l_lam"]:PVC["l_lam"] + 8], AF.Sigmoid)
        act(CTMP, CTMP, AF.Ln)
        act(CL[:, 0:8], CTMP, AF.Identity, scale=4.0)
        act(CL[:, 8:16], CTMP, AF.Identity, scale=8.0)
        act(HB, PV[:, PVC["l_ba"]:PVC["l_ba"] + 16], AF.Identity, scale=0.5)
        act(PH, PV[:, PVC["gn_g"]:PVC["gn_g"] + 4], AF.Identity, scale=0.5)
        act(CW2, PV[:, PVC["conv_w"]:PVC["conv_w"] + 62], AF.Identity, scale=0.5)
        if l == 0:
            norm_H("g_mix", 0, W)

        state["jr"] = JUNK_RATIO
        o = C_OVL
        GLU = [bf16v(o + c * 1106, 2212) for c in range(2)]
        YA = [bf16v(o + c * 1092, 2184) for c in range(2)]
        o += 2212
        DIAG = [bf16v(o + c * 1984, 31 * 128).rearrange("p (k m) -> p k m", k=31) for c in range(2)]
        o += 2 * 1984
        for c in range(2):
            for k in range(31):
                ts(DIAG[c][:, k, :], IDENT, CW2[:, c * 31 + k:c * 31 + k + 1], 0.0, ALU.mult, ALU.add, eng="pool")
        CNV = [f32v(o + c * 2180, 2180) for c in range(2)]
        o += 2 * 2180
        SQA = bf16v(o, 1024).rearrange("p (k t) -> p k t", k=2)
        o += 512
        SG = f32v(o, 512)
        RS0 = f32v(o, 512)
        RS1 = f32v(o + 512, 512)
        o += 1024
        RSV = f32v(o, 2180)
        WA = [bf16v(o + i * 512, 1024).rearrange("p (k m) -> p k m", k=8) for i in range(4)]
        for i in range(4):
            wload(bf16v(o + i * 512, 1024), win_d[l, i])
        o += 2180
        WO_A = bf16v(o, 2048).rearrange("p (k n) -> p k n", k=2)
        wload(bf16v(o, 2048), wout_d[l, :, 0:2048])
        o += 1024
        WUV = bf16v(o, 4096).rearrange("p (k n) -> p k n", k=8)
        wload(bf16v(o, 2048), wuv_d[l, :, 0:2048])
        wload(bf16v(o + 1024, 2048), wuv_d[l, :, 2048:4096])
        o += 2048
        WST = bf16v(o, 512).rearrange("p (h q) -> p h q", h=4)
        wload(bf16v(o, 512), wst_d[l])
        o += 256
        UV = [f32v(o + i * 512, 512) for i in range(2)]
        o += 1024
        VN = [bf16v(o + i * 128, 256) for i in range(2)]
        o += 256
        YC = [f32v(o + i * 256, 256) for i in range(2)]
        o += 512
        JK = f32v(o, 256)
        o += 256
        YCN = [bf16v(o + i * 128, 256) for i in range(2)]
        o += 256
        YCF = [bf16v(o + c * (W // 2), W) for c in range(2)]
        o += W
        WO_C = bf16v(o, 2048).rearrange("p (k n) -> p k n", k=2)
        wload(bf16v(o, 2048), wout_d[l, :, 6144:8192])
        o += 1024
        STC = [f32v(o + i * 16, 16) for i in range(2)]
        o += 32
        assert o <= ARENA, o

        def wout_gen(WO, nk, Ytiles):
            for m in range(8):
                for (t0, n) in blocks(0, NMl):
                    ps = bank()
                    for k in range(nk):
                        mm(ps[:, 0:n], WO[:, k, m * 128:(m + 1) * 128], Ytiles[k][:, t0:t0 + n], k == 0, k == nk - 1)
                    tt(X[:, m, t0:t0 + n], X[:, m, t0:t0 + n], ps[:, 0:n], ALU.add)
                    yield

        def gen_A():
            for c in range(2):
                memset(GLU[c][:, 0:15], 0.0)
            for c in range(2):
                for (t0, n) in blocks(0, ZA):
                    psg = bank()
                    for k in range(8):
                        mm(psg[:, 0:n], WA[2 + c][:, k, :], H[:, k, 1 + t0:1 + t0 + n], k == 0, k == 7)
                    act(SG[:, 0:n], psg[:, 0:n], AF.Tanh, scale=0.5)
                    psv = bank()
                    for k in range(8):
                        mm(psv[:, 0:n], WA[c][:, k, :], H[:, k, 1 + t0:1 + t0 + n], k == 0, k == 7)
                    stt(GLU[c][:, 15 + t0:15 + t0 + n], SG[:, 0:n], 1.0, psv[:, 0:n], ALU.add, ALU.mult)
                    yield
            def conv_blk(c, t0, n):
                ps = bank()
                for k in range(31):
                    mm(ps[:, 0:n], DIAG[c][:, k, :], GLU[c][:, t0 + k:t0 + k + n], k == 0, k == 30)
                act(CNV[c][:, t0:t0 + n], ps[:, 0:n], AF.Identity, bias=pv("conv_b", c))

            def gn_blk(c, t0, n):
                xin = CNV[c][:, t0:t0 + n]
                psm = bank()
                mm(psm[:, 0:n], BD64, xin, True, True)
                act(RS0[:, 0:n], xin, AF.Square)
                psq = bank()
                mm(psq[:, 0:n], BD64, RS0[:, 0:n], True, True)
                act(RS1[:, 0:n], psm[:, 0:n], AF.Square, scale=1.0 / 64)
                stt(RSV[:, t0:t0 + n], psq[:, 0:n], 1.0 / 64, RS1[:, 0:n], ALU.mult, ALU.subtract)
                stt(xin, psm[:, 0:n], -1.0 / 64, xin, ALU.mult, ALU.add)

            def gn_wide(c):
                cw = CNV[c][:, 0:NMl]
                rv = RSV[:, 0:NMl]
                rsqrt(rv, rv, 1.0)
                yield
                tt(cw, cw, rv, ALU.mult)
                yield
                act(rv, cw, AF.Tanh, bias=PH[:, 2 + c:3 + c], scale=PH[:, c:c + 1])
                ts(cw, cw, PH[:, c:c + 1], PH[:, 2 + c:3 + c], ALU.mult, ALU.add)
                yield
                stt(cw, rv, 1.0, cw, ALU.add, ALU.mult)
                yield

            for (t0, n) in blocks(0, NMl):
                conv_blk(0, t0, n)
                yield
            for (t0, n) in blocks(0, NMl):
                conv_blk(1, t0, n)
                yield
                gn_blk(0, t0, n)
                yield
            yield from gn_wide(0)
            for (t0, n) in blocks(0, NMl):
                gn_blk(1, t0, n)
                yield
            yield from gn_wide(1)
            for (t0, n) in blocks(0, NMl):
                ps = bank()
                for c in range(2):
                    act(SQA[:, c, 0:n], CNV[c][:, t0:t0 + n], AF.Square)
                for c in range(2):
                    mm(ps[:, 0:n], ONESB, SQA[:, c, 0:n], c == 0, c == 1)
                act(RSV[:, t0:t0 + n], ps[:, 0:n], AF.Identity, scale=1.0 / 256)
                yield
            rsqrt(RSV[:, 0:NMl], RSV[:, 0:NMl], 1.0)
            yield
            for c in range(2):
                stt(YA[c][:, 0:NMl], CNV[c][:, 0:NMl], pv("g_out", c), RSV[:, 0:NMl], ALU.mult, ALU.mult)
                yield
            yield from wout_gen(WO_A, 2, YA)

        c_done = []

        def gen_C(par):
            for ci in range(par, W // 128, 2):
                t0 = ci * 128
                q = ci % 2
                ST = STC[q]
                ps = bank()
                for k in range(8):
                    mm(ps[:, 0:512], H[:, k, 1 + t0:1 + t0 + 128], WUV[:, k, :], k == 0, k == 7)
                act(UV[q], ps[:, 0:512], AF.Gelu_apprx_tanh)
                yield
                v = UV[q][:, 256:512]
                u = UV[q][:, 0:256]
                P.op("dve", lambda v=v, ST=ST: nc.vector.bn_stats(out=ST[:, 0:6], in_=v), reads=[v], writes=[ST[:, 0:6]])
                P.op("dve", lambda ST=ST: nc.vector.bn_aggr(out=ST[:, 6:8], in_=ST[:, 0:6]), reads=[ST[:, 0:6]], writes=[ST[:, 6:8]])
                yield
                ts(ST[:, 8:9], ST[:, 7:8], EPS, 0.0, ALU.add, ALU.add)
                P.op("pool", lambda ST=ST: nc.gpsimd.tensor_tensor(out=ST[:, 9:10], in0=ST[:, 8:9], in1=NEGH, op=ALU.pow),
                     reads=[ST[:, 8:9], NEGH], writes=[ST[:, 9:10]])
                yield
                ts(YC[q], v, ST[:, 6:7], ST[:, 9:10], ALU.subtract, ALU.mult)
                tt(YC[q], YC[q], TMB[:, 0:256], ALU.mult)
                tt(VN[q], YC[q], TMB[:, 256:512], ALU.add)
                yield
                ps2 = bank()
                for h in range(4):
                    mm(ps2[:, 64 * h:64 * h + 64], WST[:, h, :], VN[q][:, 64 * h:64 * h + 64], True, True)
                tt(YC[q], ps2[:, 0:256], TMB[:, 512:768], ALU.add)
                tt(YC[q], YC[q], u, ALU.mult)
                yield
                act(JK, YC[q], AF.Square, accum=ST[:, 10:11])
                ts(ST[:, 11:12], ST[:, 10:11], 1.0 / 256, EPS, ALU.mult, ALU.add)
                P.op("pool", lambda ST=ST: nc.gpsimd.tensor_tensor(out=ST[:, 12:13], in0=ST[:, 11:12], in1=NEGH, op=ALU.pow),
                     reads=[ST[:, 11:12], NEGH], writes=[ST[:, 12:13]])
                yield
                stt(YCN[q], YC[q], ST[:, 12:13], TMB[:, 768:1024], ALU.mult, ALU.mult)
                yield
                pst = bank()
                pstb = pst[:, 0:128].bitcast(BF16)
                for j in range(2):
                    o_ap = pstb[:, j * 128:(j + 1) * 128]
                    i_ap = YCN[q][:, j * 128:(j + 1) * 128]
                    P.op("pe", lambda o_ap=o_ap, i_ap=i_ap: nc.tensor.transpose(o_ap, i_ap, IDENT),
                         reads=[i_ap, IDENT], writes=[pst[:, 0:128]], flag=(j == 1))
                for j in range(2):
                    o_ap = YCF[j][:, t0:t0 + 128]
                    i_ap = pstb[:, j * 128:(j + 1) * 128]
                    P.op("act", lambda o_ap=o_ap, i_ap=i_ap: nc.scalar.copy(out=o_ap, in_=i_ap), reads=[pst[:, 0:128]], writes=[o_ap])
                yield
            c_done.append(par)

        def gen_Cw():
            while len(c_done) < 2:
                yield
            yield from wout_gen(WO_C, 2, YCF)

        run_threads([gen_A(), gen_C(0), gen_C(1), gen_Cw()], fill=FILL_AC)

        o = C_OVL
        H0 = [f32v(o + c * WMAX, WMAX) for c in range(4)]
        o += 4 * WMAX
        yb_base = o
        NWS = 3
        LXQ = [f32v(o + i * 554, 554) for i in range(NWS)]
        o += NWS * 554
        G2 = [bf16v(o + j * (WMAX // 2), WMAX) for j in range(2)]
        o += WMAX
        ws0 = o
        WSETS = []
        for i in range(NWS):
            WSETS.append((f32v(o, 548), f32v(o + 548, 548), f32v(o + 1096, 548), f32v(o + 1644, 548)))
            o += 2192
        YB = [bf16v(yb_base + c * (WMAX // 2), WMAX) for c in range(4)]
        SQB = [bf16v(ws0 + 1200 + i * 1024, 1024).rearrange("p (k t) -> p k t", k=2) for i in range(2)]
        RSB = [f32v(ws0 + 1712 + i * 1024, 512) for i in range(2)]
        wob0 = ws0 + 3300
        WO_B = bf16v(wob0, 4096).rearrange("p (k n) -> p k n", k=4)
        tb0 = ws0 + 5348
        SQT = [bf16v(tb0 + i * 2180, 8 * 436).rearrange("p (k t) -> p k t", k=8) for i in range(2)]
        RST = [f32v(tb0 + i * 2180 + 1744, 436) for i in range(2)]
        WB = [bf16v(o + i * 512, 1024).rearrange("p (k m) -> p k m", k=8) for i in range(3)]
        WBf = [bf16v(o + i * 512, 1024) for i in range(3)]
        o += 1536
        GT_ = f32v(o, 2048).rearrange("p (u m) -> p u m", u=16)
        P.dma("sp", f32v(o, 2048), gates_d[l])
        o += 2048
        assert o <= ARENA, o
        assert tb0 + 2 * 2180 <= o

        done = set()
        tile_ready = set()
        lx_done = set()

        def unit_gen(c, d, a, b, init, out_ap, ws, lxq, first=False, post=None, uid=None, dep=None):
            XC, RA, IU, GH = ws
            n = b - a
            u = d * 4 + c
            if first:
                while c > 0 and not all((d, c - 1, q_) in lx_done for q_ in range(4)):
                    yield
                if c == 0:
                    wload(WBf[0], win_d[l, 4])
                if c + 1 < 4:
                    wload(WBf[(c + 1) % 2], win_d[l, 4 + c + 1])
                if d == 1:
                    wload(WBf[2], win_d[l, 8 + c])
                    for (t0, nn) in blocks(0, ZB):
                        ps = bank()
                        for k in range(8):
                            mm(ps[:, 0:nn], WB[2][:, k, :], H[:, k, 1 + t0:1 + t0 + nn], k == 0, k == 7)
                        act(G2[c % 2][:, t0:t0 + nn], ps[:, 0:nn], AF.Gelu_apprx_tanh)
                tile_ready.add((d, c))
            while (d, c) not in tile_ready:
                yield
            yield
            if d == 0:
                lo = max(a - 3, 0)
                hi = b
                if a == 0:
                    memset(lxq[:, 0:3], 0.0, eng="dve")
            else:
                lo, hi = a, b + 3
            for (t0, nn) in blocks(lo, hi):
                ps = bank()
                for k in range(8):
                    mm(ps[:, 0:nn], WB[c % 2][:, k, :], H[:, k, 1 + t0:1 + t0 + nn], k == 0, k == 7)
                j0 = t0 - (a - 3)
                act(lxq[:, j0:j0 + nn], ps[:, 0:nn], AF.Copy)
            lx_done.add(uid)
            yield
            for k in range(4):
                src = lxq[:, k:k + n] if d == 0 else lxq[:, 6 - k:6 - k + n]
                wk = pv("l_cw", u * 4 + k)
                if k == 0:
                    act(XC[:, 0:n], src, AF.Identity, bias=pv("l_cb", u), scale=wk)
                else:
                    stt(XC[:, 0:n], src, wk, XC[:, 0:n], ALU.mult, ALU.add)
            yield
            for (s0, sn) in blocks(0, n):
                ps = bank()
                mm(ps[:, 0:sn], GT_[:, u * 2, :], XC[:, s0:s0 + sn], True, True)
                act(RA[:, s0:s0 + sn], ps[:, 0:sn], AF.Tanh, bias=HB[:, u:u + 1], scale=0.5)
                ps = bank()
                mm(ps[:, 0:sn], GT_[:, u * 2 + 1, :], XC[:, s0:s0 + sn], True, True)
                act(IU[:, s0:s0 + sn], ps[:, 0:sn], AF.Tanh, bias=HB[:, 8 + u:9 + u], scale=0.5)
            yield
            act(GH[:, 0:n], RA[:, 0:n], AF.Exp, bias=CL[:, 8 + u:9 + u], scale=CL[:, 8 + u:9 + u])
            act(RA[:, 0:n], RA[:, 0:n], AF.Exp, bias=CL[:, u:u + 1], scale=CL[:, u:u + 1])
            stt(IU[:, 0:n], IU[:, 0:n], 1.0, XC[:, 0:n], ALU.add, ALU.mult)
            yield
            act(GH[:, 0:n], GH[:, 0:n], AF.Sqrt, bias=1.0, scale=-1.0)
            yield
            stt(IU[:, 0:n], IU[:, 0:n], 0.5, GH[:, 0:n], ALU.mult, ALU.mult)
            yield
            if d == 0:
                o_, a_, u_ = out_ap, RA[:, 0:n], IU[:, 0:n]
            else:
                o_, a_, u_ = out_ap[:, ::-1], RA[:, n - 1::-1], IU[:, n - 1::-1]
            rd = [RA[:, 0:n], IU[:, 0:n]] + ([] if isinstance(init, float) else [init])
            while dep is not None and dep not in done:
                yield
            P.op("dve", lambda: nc.vector.tensor_tensor_scan(out=o_, data0=a_, data1=u_, initial=init, op0=ALU.mult, op1=ALU.add),
                 reads=rd, writes=[out_ap])
            if post is not None:
                post()
            done.add(uid)
            yield

        def quarters(a, b):
            q = (b - a + 3) // 4
            q = -(-q // 4) * 4
            return blocks(a, b, q)

        units = []
        ui = 0
        for c in range(4):
            for qi, (a, n) in enumerate(quarters(0, NMl)):
                init = 0.0 if qi == 0 else H0[c][:, a - 1:a]
                post = (lambda c=c: act(CO[:, c:c + 1], H0[c][:, CP:CP + 1], AF.Copy)) if qi == 3 else None
                units.append(unit_gen(c, 0, a, a + n, init, H0[c][:, a:a + n], WSETS[ui % NWS], LXQ[ui % NWS],
                                      first=(qi == 0), post=post, uid=(0, c, qi), dep=((0, c, qi - 1) if qi > 0 else None)))
                ui += 1
        run_threads(units, width=NWS)
        P.dma("pool", cc_in.ap(), CO, writes=[("k", "cc_in")])
        P.collective(cc_in, cc_out, [[0, 1], [2, 3], [4, 5], [6, 7]], reads=[("k", "cc_in")], writes=[("k", "cc_out")])
        P.dma("pool", CIN2, cc_out.ap().rearrange("(r p) c -> p r c", r=2), reads=[("k", "cc_out")])
        ts(CIN, CIN2[:, 0, :], pv("mask", 0), 0.0, ALU.mult, ALU.add)
        stt(CIN, CIN2[:, 1, :], pv("mask", 1), CIN, ALU.mult, ALU.add)
        units = []
        for c in range(4):
            qs = quarters(0, D1 + 1)
            for qi in (3, 2, 1, 0):
                a, n = qs[qi]
                ws = WSETS[ui % NWS]
                init = CIN[:, c:c + 1] if qi == 3 else C1[:, c * 4 + qi + 1:c * 4 + qi + 2]

                def post(c=c, qi=qi, a=a, n=n, ws=ws):
                    GHq = ws[3]
                    act(C1[:, c * 4 + qi:c * 4 + qi + 1], GHq[:, 0:1], AF.Copy)
                    tt(H0[c][:, a:a + n], H0[c][:, a:a + n], GHq[:, 0:n], ALU.add)
                    tt(H0[c][:, a:a + n], H0[c][:, a:a + n], G2[c % 2][:, a:a + n], ALU.mult)
                units.append(unit_gen(c, 1, a, a + n, init, ws[3][:, 0:n], ws, LXQ[ui % NWS],
                                      first=(qi == 3), post=post, uid=(1, c, qi), dep=((1, c, qi + 1) if qi < 3 else None)))
                ui += 1
        run_threads(units, width=NWS)
        wload(bf16v(wob0, 2048), wout_d[l, :, 2048:4096])
        wload(bf16v(wob0 + 1024, 2048), wout_d[l, :, 4096:6144])

        def gen_tail(t0, n, sc):
            ps = bank()
            for c in range(4):
                act(SQB[sc][:, c % 2, 0:n], H0[c][:, t0:t0 + n], AF.Square)
                mm(ps[:, 0:n], ONESB, SQB[sc][:, c % 2, 0:n], c == 0, c == 3)
            rsqrt(RSB[sc][:, 0:n], ps[:, 0:n], 1.0 / 512)
            yield
            for c in range(4):
                stt(YB[c][:, t0:t0 + n], H0[c][:, t0:t0 + n], pv("g_out", 2 + c), RSB[sc][:, 0:n], ALU.mult, ALU.mult)
            yield
            for m in range(8):
                ps = bank()
                for k in range(4):
                    mm(ps[:, 0:n], WO_B[:, k, m * 128:(m + 1) * 128], YB[k][:, t0:t0 + n], k == 0, k == 3)
                tt(X[:, m, t0:t0 + n], X[:, m, t0:t0 + n], ps[:, 0:n], ALU.add)
                if m % 2 == 1:
                    yield
            for k in range(8):
                act(SQT[sc][:, k, 0:n], X[:, k, t0:t0 + n], AF.Square)
            ps = bank()
            for k in range(8):
                mm(ps[:, 0:n], ONESB, SQT[sc][:, k, 0:n], k == 0, k == 7)
            rsqrt(RST[sc][:, 0:n], ps[:, 0:n], 1.0 / D)
            yield
            for k in range(8):
                stt(H[:, k, 1 + t0:1 + t0 + n], X[:, k, t0:t0 + n], pv("g_ffn", k), RST[sc][:, 0:n], ALU.mult, ALU.mult)
            yield

        run_threads([gen_tail(t0, n, bi % 2) for bi, (t0, n) in enumerate(blocks(0, NMl))], width=2)

        state["jr"] = 0.0
        o = C_OVL
        AFN = bf16v(o, 22 * 1152).rearrange("p (j t) -> p j t", j=22)
        o += 22 * 576
        FG = f32v(o, 1160)
        FU = f32v(o + 1160, 1160)
        o += 2320
        TG = f32v(o, 1152)
        TU = f32v(o + 1152, 1152)
        o += 2304
        WUP = [[bf16v(o + (s * 2 + g) * 512, 1024) for g in range(2)] for s in range(2)]
        o += 2048
        WDN = [bf16v(o + s * 1408, 2816) for s in range(2)]
        WDNh = [[bf16v(o + s * 1408 + hh * 704, 1408) for hh in range(2)] for s in range(2)]
        o += 2816
        assert o <= ARENA, o
        for hf, (a, b) in enumerate([(0, NOl // 2), (NOl // 2, NOl)]):
            n = b - a
            for j in range(22):
                s = j % 2
                wload(WUP[s][0], wup_d[l, j])
                wload(WUP[s][1], wup_d[l, 22 + j])
                for g in range(2):
                    F = FG if g == 0 else FU
                    T = TG if g == 0 else TU
                    Wt = WUP[s][g].rearrange("p (k m) -> p k m", k=8)
                    for (s0, sn) in blocks(0, n + 2):
                        ps = bank()
                        for k in range(8):
                            mm(ps[:, 0:sn], Wt[:, k, :], H[:, k, a + s0:a + s0 + sn], k == 0, k == 7)
                        act(F[:, s0:s0 + sn], ps[:, 0:sn], AF.Copy)
                    jj = j + 22 * g
                    act(T[:, 0:n], F[:, 0:n], AF.Identity, bias=pv("f_cb", jj), scale=pv("f_cw", jj * 3))
                    stt(T[:, 0:n], F[:, 1:n + 1], pv("f_cw", jj * 3 + 1), T[:, 0:n], ALU.mult, ALU.add)
                    stt(T[:, 0:n], F[:, 2:n + 2], pv("f_cw", jj * 3 + 2), T[:, 0:n], ALU.mult, ALU.add)
                act(TG[:, 0:n], TG[:, 0:n], AF.Gelu_apprx_tanh)
                tt(AFN[:, j, 0:n], TG[:, 0:n], TU[:, 0:n], ALU.mult)
            for m in range(8):
                s = m % 2
                wload(WDNh[s][0], wdn_d[l, m, :, 0:1408])
                wload(WDNh[s][1], wdn_d[l, m, :, 1408:2816])
                Wd = WDN[s].rearrange("p (j m) -> p j m", j=22)
                for (s0, sn) in blocks(0, n):
                    ps = bank()
                    for kj in range(22):
                        mm(ps[:, 0:sn], Wd[:, kj, :], AFN[:, kj, s0:s0 + sn], kj == 0, kj == 21)
                    tt(X[:, m, a + s0:a + s0 + sn], X[:, m, a + s0:a + s0 + sn], ps[:, 0:sn], ALU.add)

        state["jr"] = JUNK_RATIO
        norm_H("g_ple", 0, NOl)
        o = C_OVL
        WG = [bf16v(o + m * 512, 1024).rearrange("p (k m) -> p k m", k=8) for m in range(8)]
        for m in range(8):
            wload(bf16v(o + m * 512, 1024), wg_d[l, m])
        o += 4096
        WPLE = bf16v(o, 2048).rearrange("p (k n) -> p k n", k=2)
        wload(bf16v(o, 2048), wple_d[l])
        o += 1024
        BW = 436
        PT = bf16v(o, 2 * BW).rearrange("p (k t) -> p k t", k=2)
        o += BW
        E2 = [f32v(o + i * 8 * BW, 8 * BW).rearrange("p (m t) -> p m t", m=8) for i in range(2)]
        o += 16 * BW
        SQE2 = [bf16v(o + i * 4 * BW, 8 * BW).rearrange("p (m t) -> p m t", m=8) for i in range(2)]
        o += 8 * BW
        GT2 = [f32v(o + i * BW, BW) for i in range(2)]
        o += 2 * BW
        RSE2 = [f32v(o + i * BW, BW) for i in range(2)]
        o += 2 * BW
        TMP = f32v(o, BW)
        o += BW
        SQN = bf16v(o, 8 * BW).rearrange("p (k t) -> p k t", k=8)
        RSN = f32v(o + 4 * BW, BW)
        o += 5 * BW
        assert o <= ARENA, o
        e_done = []
        ple_done = []
        pblocks = blocks(0, NOl, 436)

        def gen_E():
            for bi, (t0, n) in enumerate(pblocks):
                par = bi % 2
                while bi >= 2 and (bi - 2) not in ple_done:
                    yield
                P.dma("pool", PT[:, :, 0:n], p_d[l, :, :, t0:t0 + n].rearrange("k p t -> p k t"))
                for m in range(8):
                    ps = bank()
                    for ko in range(2):
                        mm(ps[:, 0:n], WPLE[:, ko, m * 128:(m + 1) * 128], PT[:, ko, 0:n], ko == 0, ko == 1)
                    act(E2[par][:, m, 0:n], ps[:, 0:n], AF.Copy)
                    act(SQE2[par][:, m, 0:n], ps[:, 0:n], AF.Square)
                    if m % 4 == 3:
                        yield
                ps = bank()
                for m in range(8):
                    mm(ps[:, 0:n], ONESB, SQE2[par][:, m, 0:n], m == 0, m == 7)
                rsqrt(RSE2[par][:, 0:n], ps[:, 0:n], 1.0 / D)
                e_done.append(bi)
                yield

        def gen_G():
            for bi, (t0, n) in enumerate(pblocks):
                par = bi % 2
                while bi not in e_done:
                    yield
                for m in range(8):
                    ps = bank()
                    for k in range(8):
                        mm(ps[:, 0:n], WG[m][:, k, :], H[:, k, 1 + t0:1 + t0 + n], k == 0, k == 7)
                    gt = GT2[m % 2]
                    act(gt[:, 0:n], ps[:, 0:n], AF.Sigmoid, bias=pv("b_gate", m))
                    stt(TMP[:, 0:n], E2[par][:, m, 0:n], pv("g_post", m), RSE2[par][:, 0:n], ALU.mult, ALU.mult)
                    tt(TMP[:, 0:n], TMP[:, 0:n], gt[:, 0:n], ALU.mult)
                    tt(X[:, m, t0:t0 + n], X[:, m, t0:t0 + n], TMP[:, 0:n], ALU.add)
                    if m % 2 == 1:
                        yield
                ple_done.append(bi)
                yield

        def gen_next_norm():
            for bi, (t0, n) in enumerate(pblocks):
                while bi not in ple_done:
                    yield
                for k in range(8):
                    act(SQN[:, k, 0:n], X[:, k, t0:t0 + n], AF.Square)
                ps = bank()
                for k in range(8):
                    mm(ps[:, 0:n], ONESB, SQN[:, k, 0:n], k == 0, k == 7)
                rsqrt(RSN[:, 0:n], ps[:, 0:n], 1.0 / D)
                yield
                if l == 0:
                    for k in range(8):
                        stt(H[:, k, 1 + t0:1 + t0 + n], X[:, k, t0:t0 + n], pv("g_next", k), RSN[:, 0:n], ALU.mult, ALU.mult)
                else:
                    for k in range(8):
                        stt(X[:, k, t0:t0 + n], X[:, k, t0:t0 + n], pv("g_next", k), RSN[:, 0:n], ALU.mult, ALU.mult)
                    finals.append(P.dma("sp", y_d[:, :, t0:t0 + n].rearrange("k p t -> p k t"), X[:, :, t0:t0 + n]))
                yield

        run_threads([gen_E(), gen_G(), gen_next_norm()])

    P.emit(finals)
    st.close()
    return nc


_CACHE = {}


def _prep_inputs(inp):
    f = lambda a: np.ascontiguousarray(a, dtype=np.float32)
    x = f(inp["x"])
    p = f(inp["p"])
    shared = {}
    w_in = f(inp["w_in"])
    shared["win_t"] = f(w_in.reshape(2, 8, 128, 16, 128).transpose(0, 3, 2, 1, 4).reshape(2, 16, 128, 1024))
    shared["wuv_t"] = f(w_in[:, :, 1536:2048].reshape(2, 8, 128, 512).transpose(0, 2, 1, 3).reshape(2, 128, 4096))
    shared["wout_t"] = f(inp["w_out"].reshape(2, 8, 128, 1024).transpose(0, 2, 1, 3).reshape(2, 128, 8192))
    shared["wup_t"] = f(inp["w_up"].reshape(2, 8, 128, 44, 128).transpose(0, 3, 2, 1, 4).reshape(2, 44, 128, 1024))
    shared["wdn_t"] = f(inp["w_down"].reshape(2, 22, 128, 8, 128).transpose(0, 3, 2, 1, 4).reshape(2, 8, 128, 2816))
    shared["wg_t"] = f(inp["w_ple_gate"].reshape(2, 8, 128, 8, 128).transpose(0, 3, 2, 1, 4).reshape(2, 8, 128, 1024))
    shared["wple_t"] = f(inp["w_ple"].reshape(2, 2, 128, 1024).transpose(0, 2, 1, 3).reshape(2, 128, 2048))
    shared["ident"] = np.eye(128, dtype=np.float32)

    def col8(v):
        return v.reshape(-1, 128).T

    maps = []
    for c in range(NCORES):
        b, side = c // 2, c % 2
        xs = x[b] if side == 0 else x[b, ::-1]
        m = dict(shared)
        m["x_fm"] = f(xs[:WMAX].T.reshape(8, 128, WMAX))
        ps_ = p[:, b] if side == 0 else p[:, b, ::-1]
        m["p_fm"] = f(ps_[:, :WMAX].transpose(0, 2, 1).reshape(2, 2, 128, WMAX))
        pvv = np.zeros((2, 128, 512), np.float32)
        tmb = np.zeros((2, 128, 1024), np.float32)
        gates = np.zeros((2, 128, 16, 128), np.float32)
        wst = np.zeros((2, 128, 4, 128), np.float32)
        for l in range(2):
            def put(name, arr):
                arr = np.asarray(arr, np.float32)
                pvv[l, :, PVC[name]:PVC[name] + arr.shape[1]] = arr
            put("g_mix", col8(inp["norm_mix"][l]))
            put("g_ffn", col8(inp["norm_ffn"][l]))
            put("g_ple", col8(inp["norm_ple"][l]))
            put("g_post", col8(inp["ple_post_norm"][l]))
            put("g_out", col8(inp["out_norm"][l]))
            put("b_gate", col8(inp["b_ple_gate"][l]))
            put("g_final", col8(inp["final_norm"]))
            put("g_next", col8(inp["norm_mix"][1]) if l == 0 else col8(inp["final_norm"]))
            cw = np.asarray(inp["conv_dw_w"][l])
            if side == 1:
                cw = cw[::-1]
            put("conv_w", np.concatenate([cw[:, cc * 128:(cc + 1) * 128].T for cc in range(2)], axis=1))
            put("conv_b", col8(inp["conv_dw_b"][l]))
            put("gn_g", col8(inp["conv_gn_g"][l]))
            put("gn_b", col8(inp["conv_gn_b"][l]))
            lcw = np.zeros((128, 32), np.float32)
            for d in range(2):
                rd = d if side == 0 else 1 - d
                for cc in range(4):
                    for k in range(4):
                        lcw[:, (d * 4 + cc) * 4 + k] = inp["lru_conv_w"][l, rd, k, cc * 128:(cc + 1) * 128]
                    for g, nm in enumerate(("lru_wa", "lru_wx")):
                        for i in range(2):
                            gates[l, 64 * i:64 * i + 64, (d * 4 + cc) * 2 + g, 64 * i:64 * i + 64] = inp[nm][l, rd, 2 * cc + i]
            put("l_cw", lcw)
            dirs = [0, 1] if side == 0 else [1, 0]
            for nm, key in (("l_cb", "lru_conv_b"), ("l_ba", "lru_ba"), ("l_bx", "lru_bx"), ("l_lam", "lru_lambda")):
                put(nm, np.concatenate([col8(inp[key][l, rd]) for rd in dirs], axis=1))
            fw = np.asarray(inp["ffn_conv_w"][l])
            if side == 1:
                fw = fw[::-1]
            put("f_cw", fw.reshape(3, 44, 128).transpose(2, 1, 0).reshape(128, 132))
            put("f_cb", col8(inp["ffn_conv_b"][l]))
            put("mask", np.broadcast_to(np.array([[1.0, 0.0]] if side == 1 else [[0.0, 1.0]], np.float32), (128, 2)))
            tmb[l, :, 0:256] = inp["sgu_ln_g"][l][None, :]
            tmb[l, :, 256:512] = inp["sgu_ln_b"][l][None, :]
            bs = np.asarray(inp["sgu_bs"][l])
            ws = np.asarray(inp["sgu_ws"][l])
            if side == 1:
                bs = bs[:, ::-1]
                ws = ws[:, ::-1, ::-1]
            tmb[l, :, 512:768] = np.repeat(bs.T, 64, axis=1)
            tmb[l, :, 768:1024] = inp["out_norm"][l][None, 768:1024]
            wst[l] = ws.transpose(2, 0, 1)
        m["pv"] = pvv
        m["tmb"] = tmb
        m["gates_t"] = f(gates.reshape(2, 128, 2048))
        m["wst_t"] = f(wst.reshape(2, 128, 512))
        maps.append(m)
    return maps


def kernel(**inputs):
    if "nc" not in _CACHE:
        _CACHE["nc"] = build_program()
    nc = _CACHE["nc"]
    maps = _prep_inputs(inputs)
    res = run_bass_kernel_spmd(nc, maps, core_ids=list(range(NCORES)))
    out = np.zeros((4, SEQ, D), np.float32)
    for c in range(NCORES):
        b, side = c // 2, c % 2
        y = np.asarray(res.results[c]["y_fm"]).reshape(D, OWN).T
        if side == 0:
            out[b, 0:OWN] = y
        else:
            out[b, OWN:] = y[::-1]
    return out
```

```python
import numpy as np
from contextlib import ExitStack
import concourse.bass as bass
import concourse.mybir as mybir

F32 = mybir.dt.float32
BF16 = mybir.dt.bfloat16
AF = mybir.ActivationFunctionType
ALU = mybir.AluOpType


SAME_ENG_GAP = 1000


class Ref:
    __slots__ = ("eng", "sem", "tick")

    def __init__(self, eng, sem=None, tick=None):
        self.eng, self.sem, self.tick = eng, sem, tick


def ap_range(ap):
    if isinstance(ap, tuple):
        return ap[1], 0, 1
    t = ap.tensor
    if type(t).__name__.startswith("DRam"):
        return None
    isz = mybir.dt.size(ap.dtype)
    dims = list(ap.ap)
    pstep = dims[0][0]
    off = ap.offset % pstep if pstep > 0 else ap.offset
    lo = hi = off
    for st, cnt in dims[1:]:
        if st >= 0:
            hi += st * (cnt - 1)
        else:
            lo += st * (cnt - 1)
    return t.name, lo * isz, (hi + 1) * isz


class Prog:
    NRING = 8

    def __init__(self, nc):
        self.nc = nc
        self.engs = {"pe": nc.tensor, "act": nc.scalar, "dve": nc.vector,
                     "pool": nc.gpsimd, "sp": nc.sync}
        self.ops = {e: [] for e in self.engs}
        self.count = {e: 0 for e in self.engs}
        self.waited = {e: {} for e in self.engs}
        self.acc = {}
        self.pe_pending = []
        self.ndma = {"sp": 0, "pool": 0, "act": 0}
        self.ncc = 0
        self.psum = []
        self.ps_i = 0

    def _collect(self, reads, writes):
        deps = []
        for ap in reads:
            r = ap_range(ap)
            if r is None:
                continue
            name, lo, hi = r
            for (l2, h2, ref, w) in self.acc.get(name, ()):
                if w and l2 < hi and lo < h2:
                    deps.append(ref)
        for ap in writes:
            r = ap_range(ap)
            if r is None:
                continue
            name, lo, hi = r
            for (l2, h2, ref, w) in self.acc.get(name, ()):
                if l2 < hi and lo < h2:
                    deps.append(ref)
        return deps

    def _update(self, reads, writes, ref):
        for ap in writes:
            r = ap_range(ap)
            if r is None:
                continue
            name, lo, hi = r
            lst = self.acc.setdefault(name, [])
            lst[:] = [e for e in lst if not (lo <= e[0] and e[1] <= hi)]
            lst.append((lo, hi, ref, True))
        for ap in reads:
            r = ap_range(ap)
            if r is None:
                continue
            name, lo, hi = r
            lst = self.acc.setdefault(name, [])
            if ref.eng != "dma":
                lst[:] = [e for e in lst if not (e[0] == lo and e[1] == hi and (not e[3]) and e[2].eng == ref.eng)]
            lst.append((lo, hi, ref, False))

    def _flush_pe(self):
        if not self.pe_pending:
            return
        ent = None
        for e_ in reversed(self.ops["pe"]):
            if e_[0] == "ins":
                ent = e_
                break
        if not ent[3]:
            ent[3] = True
            self.count["pe"] += 1
        for r in self.pe_pending:
            r.tick = self.count["pe"]
        self.pe_pending = []

    def _wait(self, eng, ref):
        if ref.eng == eng and eng == "pe":
            return
        if ref.eng == "pe" and ref.tick is None:
            self._flush_pe()
        if ref.eng == eng:
            if eng != "sp" and self.count[eng] - ref.tick >= SAME_ENG_GAP:
                return
        sem, val = ref.sem, ref.tick
        if self.waited[eng].get(sem, 0) >= val:
            return
        self.waited[eng][sem] = val
        self.ops[eng].append(["wait", sem, val])

    def op(self, eng, fn, reads=(), writes=(), flag=True):
        for ref in self._collect(reads, writes):
            self._wait(eng, ref)
        ref = Ref(eng, "S_" + eng)
        if eng == "pe":
            self.pe_pending.append(ref)
            self.ops[eng].append(["ins", fn, ref, False])
            if flag:
                self._flush_pe()
        else:
            self.count[eng] += 1
            ref.tick = self.count[eng]
            self.ops[eng].append(["ins", fn, ref, True])
        self._update(reads, writes, ref)
        return ref

    def dma(self, q, out, in_, reads=None, writes=None, **kw):
        nc = self.nc
        reads = [in_] if reads is None else reads
        writes = [out] if writes is None else writes
        for ref in self._collect(reads, writes):
            self._wait(q, ref)
        i = self.ndma[q]
        self.ndma[q] += 1
        ring = "D_%s_%d" % (q, i % self.NRING)
        val = 16 * (i // self.NRING + 1)
        if val > 16:
            self._wait(q, Ref("dma", ring, val - 16))
        ref = Ref("dma", ring, val)
        eng = self.engs[q]
        self.ops[q].append(["dma", (lambda: eng.dma_start(out=out, in_=in_, **kw)), ring])
        self._update(reads, writes, ref)
        return ref

    def collective(self, in_dram, out_dram, groups, reads, writes):
        nc = self.nc
        for ref in self._collect(reads, writes):
            self._wait("pool", ref)
        self.ncc += 1
        ref = Ref("cc", "S_cc", self.ncc)
        self.ops["pool"].append(["cc", (lambda: nc.gpsimd.collective_compute(
            "AllGather", ALU.bypass, replica_groups=groups,
            ins=[in_dram.ap().opt()], outs=[out_dram.ap().opt()])), "S_cc"])
        self._update(reads, writes, ref)
        return ref

    def wait_all(self, eng, refs):
        for r in refs:
            self._wait(eng, r)

    def emit(self, final_refs):
        nc = self.nc
        self._flush_pe()
        for r in final_refs:
            self._wait("sp", r)
        names = ["S_pe", "S_act", "S_dve", "S_pool", "S_sp", "S_cc"]
        for q in ("sp", "pool", "act"):
            names += ["D_%s_%d" % (q, i) for i in range(self.NRING)]
        with ExitStack() as st:
            sems = {n: st.enter_context(nc.semaphore(n)) for n in names}
            block = st.enter_context(nc.Block())
            prog = self

            def replay(eng_name, eng):
                for ent in prog.ops[eng_name]:
                    if ent[0] == "wait":
                        eng.wait_ge(sems[ent[1]], ent[2])
                    elif ent[0] == "ins":
                        ins = ent[1]()
                        if ent[3]:
                            ins.then_inc(sems["S_" + eng_name], 1)
                    elif ent[0] == "dma":
                        ent[1]().then_inc(sems[ent[2]], 16)
                    elif ent[0] == "cc":
                        ent[1]().then_inc(sems[ent[2]])

            @block.tensor
            def _(e):
                replay("pe", nc.tensor)

            @block.scalar
            def _(e):
                replay("act", nc.scalar)

            @block.vector
            def _(e):
                replay("dve", nc.vector)

            @block.gpsimd
            def _(e):
                replay("pool", nc.gpsimd)

            @block.sync
            def _(e):
                replay("sp", nc.sync)

from concourse.bass_utils import run_bass_kernel_spmd

D = 1024
SEQ = 4096
OWN = 2048
WL = [2304, 2176]
WMAX = 2304
NM = [2180, 2052]
NO = [2176, 2048]
D1S = [2183, 2055]
CPART = [SEQ - 1 - (d + 1) for d in D1S]
EPS = 1e-6
NCORES = 8
FILL_AC = 0
JUNK_RATIO = 0.0
FILL_B = 0

C_X = 0
C_H = C_X + 8 * WMAX
HW = WMAX + 2
C_PV = C_H + 8 * HW // 2
C_TMB = C_PV + 512
C_CONST = C_TMB + 1024
C_OVL = C_CONST + 512
ARENA = 53200

PVC = {}
_o = 0
for _n, _w in [("g_mix", 8), ("g_ffn", 8), ("g_ple", 8), ("g_post", 8), ("g_out", 8), ("b_gate", 8),
               ("g_final", 8), ("conv_w", 62), ("conv_b", 2), ("gn_g", 2), ("gn_b", 2),
               ("l_cw", 32), ("l_cb", 8), ("l_ba", 8), ("l_bx", 8), ("l_lam", 8),
               ("f_cw", 132), ("f_cb", 44), ("mask", 2), ("g_next", 8)]:
    PVC[_n] = _o
    _o += _w
NPV = _o
assert NPV <= 512


def blocks(a, b, step=512):
    if step == 512 and (b - a) % 512 != 0:
        nb = -(-(b - a) // 512)
        step = -(-(b - a) // nb)
        step = -(-step // 4) * 4
    out = []
    t = a
    while t < b:
        n = min(step, b - t)
        out.append((t, n))
        t += n
    return out


def build_program():
    nc = bass.Bass("TRN2", target_bir_lowering=False)
    dt = lambda name, shape, kind="ExternalInput": nc.dram_tensor(name, shape, F32, kind=kind)
    x_d = dt("x_fm", [8, 128, WMAX]).ap()
    p_d = dt("p_fm", [2, 2, 128, WMAX]).ap()
    pv_d = dt("pv", [2, 128, 512]).ap()
    tmb_d = dt("tmb", [2, 128, 1024]).ap()
    ident_d = dt("ident", [128, 128]).ap()
    win_d = dt("win_t", [2, 16, 128, 1024]).ap()
    wuv_d = dt("wuv_t", [2, 128, 4096]).ap()
    wout_d = dt("wout_t", [2, 128, 8192]).ap()
    wup_d = dt("wup_t", [2, 44, 128, 1024]).ap()
    wdn_d = dt("wdn_t", [2, 8, 128, 2816]).ap()
    wg_d = dt("wg_t", [2, 8, 128, 1024]).ap()
    wple_d = dt("wple_t", [2, 128, 2048]).ap()
    gates_d = dt("gates_t", [2, 128, 2048]).ap()
    wst_d = dt("wst_t", [2, 128, 512]).ap()
    y_d = dt("y_fm", [8, 128, OWN], kind="ExternalOutput").ap()
    cc_in = nc.dram_tensor("cc_in", [128, 4], F32)
    cc_out = nc.dram_tensor("cc_out", [256, 4], F32)

    st = ExitStack()
    AR = st.enter_context(nc.sbuf_tensor("arena", [128, ARENA], F32))
    PS = [st.enter_context(nc.psum_tensor("ps%d" % i, [128, 512], F32)) for i in range(8)]
    P = Prog(nc)
    state = {"ps": 0}

    def bank():
        b = PS[state["ps"] % (7 if (FILL_AC or FILL_B or JUNK_RATIO > 0) else 8)]
        state["ps"] += 1
        ents = P.acc.get(b.name, [])
        assert not (ents and ents[-1][3]), "PSUM bank %s reallocated before its contents were consumed" % b.name
        return b

    def f32v(c0, n):
        return AR[:, c0:c0 + n]

    def bf16v(c0, n):
        return AR[:, c0:c0 + (n + 1) // 2].bitcast(BF16)

    X = f32v(C_X, 8 * WMAX).rearrange("p (k t) -> p k t", k=8)
    H = bf16v(C_H, 8 * HW).rearrange("p (k t) -> p k t", k=8)
    PV = f32v(C_PV, 512)
    TMB = f32v(C_TMB, 1024)
    ONESB = bf16v(C_CONST, 128)
    IDENT = bf16v(C_CONST + 64, 128)
    BD64 = f32v(C_CONST + 128, 128)
    CO = f32v(C_CONST + 256, 4)
    CIN2 = f32v(C_CONST + 260, 8).rearrange("p (r c) -> p r c", r=2)
    CIN = f32v(C_CONST + 268, 4)
    CL = f32v(C_CONST + 272, 16)
    C1 = f32v(C_CONST + 340, 20)
    ST = f32v(C_CONST + 296, 32)
    CTMP = f32v(C_CONST + 328, 8)
    NEGH = f32v(C_CONST + 364, 1)
    HB = f32v(C_CONST + 368, 16)
    PH = f32v(C_CONST + 384, 4)
    CW2 = f32v(C_CONST + 392, 62)

    def pv(name, i=0):
        c = PVC[name] + i
        return PV[:, c:c + 1]

    def act(out, in_, func, bias=None, scale=None, accum=None):
        rd = [in_]
        kw = {}
        if bias is not None:
            kw["bias"] = bias
            if not isinstance(bias, float):
                rd.append(bias)
        if scale is not None:
            kw["scale"] = scale
            if not isinstance(scale, float):
                rd.append(scale)
        wr = [out]
        if accum is not None:
            kw["accum_out"] = accum
            wr.append(accum)
        P.op("act", lambda: nc.scalar.activation(out=out, in_=in_, func=func, **kw), reads=rd, writes=wr)

    def tt(out, a, b, op, eng="dve"):
        e = nc.vector if eng == "dve" else nc.gpsimd
        P.op(eng, lambda: e.tensor_tensor(out=out, in0=a, in1=b, op=op), reads=[a, b], writes=[out])

    def stt(out, in0, scalar, in1, op0, op1):
        rd = [in0, in1] + ([] if isinstance(scalar, float) else [scalar])
        P.op("dve", lambda: nc.vector.scalar_tensor_tensor(out=out, in0=in0, scalar=scalar, in1=in1, op0=op0, op1=op1),
             reads=rd, writes=[out])

    def ts(out, in0, s1, s2, op0, op1, eng="dve"):
        e = nc.vector if eng == "dve" else nc.gpsimd
        rd = [in0] + [s for s in (s1, s2) if not isinstance(s, float)]
        P.op(eng, lambda: e.tensor_scalar(out=out, in0=in0, scalar1=s1, scalar2=s2, op0=op0, op1=op1), reads=rd, writes=[out])

    TMBJ = TMB.bitcast(BF16)[:, 0:512]

    def mm(ps, lhsT, rhs, start, stop):
        P.op("pe", lambda: nc.tensor.matmul(ps, lhsT, rhs, start=start, stop=stop), reads=[lhsT, rhs], writes=[ps], flag=stop)
        if JUNK_RATIO > 0:
            state["grp"] = state.get("grp", 0) + 1
            if stop:
                state["jacc"] = state.get("jacc", 0.0) + state["grp"] * state.get("jr", 0.0)
                state["grp"] = 0
                while state["jacc"] >= 1.0:
                    state["jacc"] -= 1.0
                    P.op("pe", lambda: nc.tensor.matmul(PS[7][:, 0:512], ONESB, TMBJ, start=True, stop=True),
                         reads=[ONESB, TMBJ], writes=[PS[7][:, 0:512]], flag=False)

    def memset(ap, val, eng="pool"):
        e = nc.vector if eng == "dve" else nc.gpsimd
        P.op(eng, lambda: e.memset(ap, val), writes=[ap])

    def wload(dst_bf16, src_dram):
        P.dma("pool", dst_bf16, src_dram)

    def rsqrt(dst, src, scale):
        act(dst, src, AF.Ln, bias=EPS, scale=scale)
        act(dst, dst, AF.Exp, scale=-0.5)

    memset(H[:, :, 0:1], 0.0, eng="dve")
    memset(H[:, :, WMAX + 1:WMAX + 2], 0.0, eng="dve")
    memset(AR[:, C_OVL:C_OVL + 12000], 0.0, eng="dve")
    memset(AR[:, C_OVL + 12000:ARENA], 0.0, eng="pool")
    memset(ONESB, 1.0, eng="dve")
    memset(BD64, 0.0, eng="dve")
    memset(BD64[0:64, 0:64], 1.0, eng="dve")
    memset(BD64[64:128, 64:128], 1.0, eng="dve")
    wload(IDENT, ident_d)
    memset(NEGH, -0.5, eng="dve")

    NSQ = C_OVL + 20000

    def norm_H(gname, t_lo, t_hi):
        SQ = bf16v(NSQ, 8 * 512).rearrange("p (k t) -> p k t", k=8)
        RS = f32v(NSQ + 2048, 512)
        for (t0, n) in blocks(t_lo, t_hi):
            for k in range(8):
                act(SQ[:, k, 0:n], X[:, k, t0:t0 + n], AF.Square)
            ps = bank()
            for k in range(8):
                mm(ps[:, 0:n], ONESB, SQ[:, k, 0:n], k == 0, k == 7)
            rsqrt(RS[:, 0:n], ps[:, 0:n], 1.0 / D)
            for k in range(8):
                stt(H[:, k, 1 + t0:1 + t0 + n], X[:, k, t0:t0 + n], pv(gname, k), RS[:, 0:n], ALU.mult, ALU.mult)

    def wout_apply(WO, nk, Ytiles, W):
        for m in range(8):
            for (t0, n) in blocks(0, W):
                ps = bank()
                for k in range(nk):
                    mm(ps[:, 0:n], WO[:, k, m * 128:(m + 1) * 128], Ytiles[k][:, t0:t0 + n], k == 0, k == nk - 1)
                tt(X[:, m, t0:t0 + n], X[:, m, t0:t0 + n], ps[:, 0:n], ALU.add)

    def pe_filler(k):
        for _ in range(k):
            P.op("pe", lambda: nc.tensor.matmul(PS[7][:, 0:128], ONESB, IDENT, start=True, stop=True),
                 reads=[ONESB, IDENT], writes=[PS[7][:, 0:128]], flag=False)

    def run_threads(gens, width=None, fill=0):
        pending = list(gens)
        active = []
        width = width or len(pending)
        while pending or active:
            while pending and len(active) < width:
                active.append(pending.pop(0))
            for g in list(active):
                try:
                    next(g)
                except StopIteration:
                    active.remove(g)
            if fill:
                pe_filler(fill)

    finals = []
    for l in range(2):
        W = WL[l]
        NMl, NOl, D1, CP = NM[l], NO[l], D1S[l], CPART[l]
        ZA, ZB = NMl + 16, D1 + 5
        P.dma("sp", PV, pv_d[l])
        P.dma("sp", TMB, tmb_d[l])
        if l == 0:
            for bi_, (t0, n) in enumerate(blocks(0, WMAX)):
                P.dma("sp" if bi_ % 2 == 0 else "act", X[:, :, t0:t0 + n], x_d[:, :, t0:t0 + n].rearrange("k p t -> p k t"))
        act(CTMP, PV[:, PVC["l_lam"]:PVC["l_lam"] + 8], AF.Sigmoid)
        act(CTMP, CTMP, AF.Ln)
        act(CL[:, 0:8], CTMP, AF.Identity, scale=4.0)
        act(CL[:, 8:16], CTMP, AF.Identity, scale=8.0)
        act(HB, PV[:, PVC["l_ba"]:PVC["l_ba"] + 16], AF.Identity, scale=0.5)
        act(PH, PV[:, PVC["gn_g"]:PVC["gn_g"] + 4], AF.Identity, scale=0.5)
        act(CW2, PV[:, PVC["conv_w"]:PVC["conv_w"] + 62], AF.Identity, scale=0.5)
        norm_H("g_mix", 0, W)

        state["jr"] = JUNK_RATIO
        o = C_OVL
        GLU = [bf16v(o + c * 1106, 2212) for c in range(2)]
        YA = [bf16v(o + c * 1092, 2184) for c in range(2)]
        o += 2212
        DIAG = [bf16v(o + c * 1984, 31 * 128).rearrange("p (k m) -> p k m", k=31) for c in range(2)]
        o += 2 * 1984
        for c in range(2):
            for k in range(31):
                ts(DIAG[c][:, k, :], IDENT, CW2[:, c * 31 + k:c * 31 + k + 1], 0.0, ALU.mult, ALU.add, eng="pool")
        CNV = [f32v(o + c * 2180, 2180) for c in range(2)]
        o += 2 * 2180
        SQA = bf16v(o, 1024).rearrange("p (k t) -> p k t", k=2)
        o += 512
        SG = f32v(o, 512)
        RS0 = f32v(o, 512)
        RS1 = f32v(o + 512, 512)
        o += 1024
        RSV = f32v(o, 2180)
        WA = [bf16v(o + i * 512, 1024).rearrange("p (k m) -> p k m", k=8) for i in range(4)]
        for i in range(4):
            wload(bf16v(o + i * 512, 1024), win_d[l, i])
        o += 2180
        WO_A = bf16v(o, 2048).rearrange("p (k n) -> p k n", k=2)
        wload(bf16v(o, 2048), wout_d[l, :, 0:2048])
        o += 1024
        WUV = bf16v(o, 4096).rearrange("p (k n) -> p k n", k=8)
        wload(bf16v(o, 2048), wuv_d[l, :, 0:2048])
        wload(bf16v(o + 1024, 2048), wuv_d[l, :, 2048:4096])
        o += 2048
        WST = bf16v(o, 512).rearrange("p (h q) -> p h q", h=4)
        wload(bf16v(o, 512), wst_d[l])
        o += 256
        UV = [f32v(o + i * 512, 512) for i in range(2)]
        o += 1024
        VN = [bf16v(o + i * 128, 256) for i in range(2)]
        o += 256
        YC = [f32v(o + i * 256, 256) for i in range(2)]
        o += 512
        JK = f32v(o, 256)
        o += 256
        YCN = [bf16v(o + i * 128, 256) for i in range(2)]
        o += 256
        YCF = [bf16v(o + c * (W // 2), W) for c in range(2)]
        o += W
        WO_C = bf16v(o, 2048).rearrange("p (k n) -> p k n", k=2)
        wload(bf16v(o, 2048), wout_d[l, :, 6144:8192])
        o += 1024
        STC = [f32v(o + i * 16, 16) for i in range(2)]
        o += 32
        assert o <= ARENA, o

        def wout_gen(WO, nk, Ytiles):
            for m in range(8):
                for (t0, n) in blocks(0, NMl):
                    ps = bank()
                    for k in range(nk):
                        mm(ps[:, 0:n], WO[:, k, m * 128:(m + 1) * 128], Ytiles[k][:, t0:t0 + n], k == 0, k == nk - 1)
                    tt(X[:, m, t0:t0 + n], X[:, m, t0:t0 + n], ps[:, 0:n], ALU.add)
                    yield

        def gen_A():
            for c in range(2):
                memset(GLU[c][:, 0:15], 0.0)
            for c in range(2):
                for (t0, n) in blocks(0, ZA):
                    psg = bank()
                    for k in range(8):
                        mm(psg[:, 0:n], WA[2 + c][:, k, :], H[:, k, 1 + t0:1 + t0 + n], k == 0, k == 7)
                    act(SG[:, 0:n], psg[:, 0:n], AF.Tanh, scale=0.5)
                    psv = bank()
                    for k in range(8):
                        mm(psv[:, 0:n], WA[c][:, k, :], H[:, k, 1 + t0:1 + t0 + n], k == 0, k == 7)
                    stt(GLU[c][:, 15 + t0:15 + t0 + n], SG[:, 0:n], 1.0, psv[:, 0:n], ALU.add, ALU.mult)
                    yield
            for c in range(2):
                for (t0, n) in blocks(0, NMl):
                    ps = bank()
                    for k in range(31):
                        mm(ps[:, 0:n], DIAG[c][:, k, :], GLU[c][:, t0 + k:t0 + k + n], k == 0, k == 30)
                    act(CNV[c][:, t0:t0 + n], ps[:, 0:n], AF.Identity, bias=pv("conv_b", c))
                    yield
                for (t0, n) in blocks(0, NMl):
                    xin = CNV[c][:, t0:t0 + n]
                    psm = bank()
                    mm(psm[:, 0:n], BD64, xin, True, True)
                    act(RS0[:, 0:n], xin, AF.Square)
                    psq = bank()
                    mm(psq[:, 0:n], BD64, RS0[:, 0:n], True, True)
                    act(RS1[:, 0:n], psm[:, 0:n], AF.Square, scale=1.0 / 64)
                    stt(RSV[:, t0:t0 + n], psq[:, 0:n], 1.0 / 64, RS1[:, 0:n], ALU.mult, ALU.subtract)
                    stt(xin, psm[:, 0:n], -1.0 / 64, xin, ALU.mult, ALU.add)
                    yield
                cw = CNV[c][:, 0:NMl]
                rv = RSV[:, 0:NMl]
                rsqrt(rv, rv, 1.0)
                yield
                tt(cw, cw, rv, ALU.mult)
                yield
                act(rv, cw, AF.Tanh, bias=PH[:, 2 + c:3 + c], scale=PH[:, c:c + 1])
                ts(cw, cw, PH[:, c:c + 1], PH[:, 2 + c:3 + c], ALU.mult, ALU.add)
                yield
                stt(cw, rv, 1.0, cw, ALU.add, ALU.mult)
                yield
            for (t0, n) in blocks(0, NMl):
                ps = bank()
                for c in range(2):
                    act(SQA[:, c, 0:n], CNV[c][:, t0:t0 + n], AF.Square)
                for c in range(2):
                    mm(ps[:, 0:n], ONESB, SQA[:, c, 0:n], c == 0, c == 1)
                act(RSV[:, t0:t0 + n], ps[:, 0:n], AF.Identity, scale=1.0 / 256)
                yield
            rsqrt(RSV[:, 0:NMl], RSV[:, 0:NMl], 1.0)
            yield
            for c in range(2):
                stt(YA[c][:, 0:NMl], CNV[c][:, 0:NMl], pv("g_out", c), RSV[:, 0:NMl], ALU.mult, ALU.mult)
                yield
            yield from wout_gen(WO_A, 2, YA)

        c_done = []

        def gen_C(par):
            for ci in range(par, W // 128, 2):
                t0 = ci * 128
                q = ci % 2
                ST = STC[q]
                ps = bank()
                for k in range(8):
                    mm(ps[:, 0:512], H[:, k, 1 + t0:1 + t0 + 128], WUV[:, k, :], k == 0, k == 7)
                act(UV[q], ps[:, 0:512], AF.Gelu_apprx_tanh)
                yield
                v = UV[q][:, 256:512]
                u = UV[q][:, 0:256]
                P.op("dve", lambda v=v, ST=ST: nc.vector.bn_stats(out=ST[:, 0:6], in_=v), reads=[v], writes=[ST[:, 0:6]])
                P.op("dve", lambda ST=ST: nc.vector.bn_aggr(out=ST[:, 6:8], in_=ST[:, 0:6]), reads=[ST[:, 0:6]], writes=[ST[:, 6:8]])
                yield
                ts(ST[:, 8:9], ST[:, 7:8], EPS, 0.0, ALU.add, ALU.add)
                P.op("pool", lambda ST=ST: nc.gpsimd.tensor_tensor(out=ST[:, 9:10], in0=ST[:, 8:9], in1=NEGH, op=ALU.pow),
                     reads=[ST[:, 8:9], NEGH], writes=[ST[:, 9:10]])
                yield
                ts(YC[q], v, ST[:, 6:7], ST[:, 9:10], ALU.subtract, ALU.mult)
                tt(YC[q], YC[q], TMB[:, 0:256], ALU.mult)
                tt(VN[q], YC[q], TMB[:, 256:512], ALU.add)
                yield
                ps2 = bank()
                for h in range(4):
                    mm(ps2[:, 64 * h:64 * h + 64], WST[:, h, :], VN[q][:, 64 * h:64 * h + 64], True, True)
                tt(YC[q], ps2[:, 0:256], TMB[:, 512:768], ALU.add)
                tt(YC[q], YC[q], u, ALU.mult)
                yield
                act(JK, YC[q], AF.Square, accum=ST[:, 10:11])
                ts(ST[:, 11:12], ST[:, 10:11], 1.0 / 256, EPS, ALU.mult, ALU.add)
                P.op("pool", lambda ST=ST: nc.gpsimd.tensor_tensor(out=ST[:, 12:13], in0=ST[:, 11:12], in1=NEGH, op=ALU.pow),
                     reads=[ST[:, 11:12], NEGH], writes=[ST[:, 12:13]])
                yield
                stt(YCN[q], YC[q], ST[:, 12:13], TMB[:, 768:1024], ALU.mult, ALU.mult)
                yield
                pst = bank()
                pstb = pst[:, 0:128].bitcast(BF16)
                for j in range(2):
                    o_ap = pstb[:, j * 128:(j + 1) * 128]
                    i_ap = YCN[q][:, j * 128:(j + 1) * 128]
                    P.op("pe", lambda o_ap=o_ap, i_ap=i_ap: nc.tensor.transpose(o_ap, i_ap, IDENT),
                         reads=[i_ap, IDENT], writes=[pst[:, 0:128]], flag=(j == 1))
                for j in range(2):
                    o_ap = YCF[j][:, t0:t0 + 128]
                    i_ap = pstb[:, j * 128:(j + 1) * 128]
                    P.op("act", lambda o_ap=o_ap, i_ap=i_ap: nc.scalar.copy(out=o_ap, in_=i_ap), reads=[pst[:, 0:128]], writes=[o_ap])
                yield
            c_done.append(par)

        def gen_Cw():
            while len(c_done) < 2:
                yield
            yield from wout_gen(WO_C, 2, YCF)

        run_threads([gen_A(), gen_C(0), gen_C(1), gen_Cw()], fill=FILL_AC)

        o = C_OVL
        H0 = [f32v(o + c * WMAX, WMAX) for c in range(4)]
        o += 4 * WMAX
        yb_base = o
        NWS = 3
        LXQ = [f32v(o + i * 554, 554) for i in range(NWS)]
        o += NWS * 554
        G2 = [bf16v(o + j * (WMAX // 2), WMAX) for j in range(2)]
        o += WMAX
        ws0 = o
        WSETS = []
        for i in range(NWS):
            WSETS.append((f32v(o, 548), f32v(o + 548, 548), f32v(o + 1096, 548), f32v(o + 1644, 548)))
            o += 2192
        YB = [bf16v(yb_base + c * (WMAX // 2), WMAX) for c in range(4)]
        SQB = [bf16v(ws0 + 1200 + i * 1024, 1024).rearrange("p (k t) -> p k t", k=2) for i in range(2)]
        RSB = [f32v(ws0 + 1712 + i * 1024, 512) for i in range(2)]
        wob0 = ws0 + 3300
        WO_B = bf16v(wob0, 4096).rearrange("p (k n) -> p k n", k=4)
        tb0 = ws0 + 5348
        SQT = [bf16v(tb0 + i * 2180, 8 * 436).rearrange("p (k t) -> p k t", k=8) for i in range(2)]
        RST = [f32v(tb0 + i * 2180 + 1744, 436) for i in range(2)]
        WB = [bf16v(o + i * 512, 1024).rearrange("p (k m) -> p k m", k=8) for i in range(3)]
        WBf = [bf16v(o + i * 512, 1024) for i in range(3)]
        o += 1536
        GT_ = f32v(o, 2048).rearrange("p (u m) -> p u m", u=16)
        P.dma("sp", f32v(o, 2048), gates_d[l])
        o += 2048
        assert o <= ARENA, o
        assert tb0 + 2 * 2180 <= o

        done = set()
        tile_ready = set()
        lx_done = set()

        def unit_gen(c, d, a, b, init, out_ap, ws, lxq, first=False, post=None, uid=None, dep=None):
            XC, RA, IU, GH = ws
            n = b - a
            u = d * 4 + c
            if first:
                while c > 0 and not all((d, c - 1, q_) in lx_done for q_ in range(4)):
                    yield
                if c == 0:
                    wload(WBf[0], win_d[l, 4])
                if c + 1 < 4:
                    wload(WBf[(c + 1) % 2], win_d[l, 4 + c + 1])
                if d == 1:
                    wload(WBf[2], win_d[l, 8 + c])
                    for (t0, nn) in blocks(0, ZB):
                        ps = bank()
                        for k in range(8):
                            mm(ps[:, 0:nn], WB[2][:, k, :], H[:, k, 1 + t0:1 + t0 + nn], k == 0, k == 7)
                        act(G2[c % 2][:, t0:t0 + nn], ps[:, 0:nn], AF.Gelu_apprx_tanh)
                tile_ready.add((d, c))
            while (d, c) not in tile_ready:
                yield
            yield
            if d == 0:
                lo = max(a - 3, 0)
                hi = b
                if a == 0:
                    memset(lxq[:, 0:3], 0.0, eng="dve")
            else:
                lo, hi = a, b + 3
            for (t0, nn) in blocks(lo, hi):
                ps = bank()
                for k in range(8):
                    mm(ps[:, 0:nn], WB[c % 2][:, k, :], H[:, k, 1 + t0:1 + t0 + nn], k == 0, k == 7)
                j0 = t0 - (a - 3)
                act(lxq[:, j0:j0 + nn], ps[:, 0:nn], AF.Copy)
            lx_done.add(uid)
            yield
            for k in range(4):
                src = lxq[:, k:k + n] if d == 0 else lxq[:, 6 - k:6 - k + n]
                wk = pv("l_cw", u * 4 + k)
                if k == 0:
                    act(XC[:, 0:n], src, AF.Identity, bias=pv("l_cb", u), scale=wk)
                else:
                    stt(XC[:, 0:n], src, wk, XC[:, 0:n], ALU.mult, ALU.add)
            yield
            for (s0, sn) in blocks(0, n):
                ps = bank()
                mm(ps[:, 0:sn], GT_[:, u * 2, :], XC[:, s0:s0 + sn], True, True)
                act(RA[:, s0:s0 + sn], ps[:, 0:sn], AF.Tanh, bias=HB[:, u:u + 1], scale=0.5)
                ps = bank()
                mm(ps[:, 0:sn], GT_[:, u * 2 + 1, :], XC[:, s0:s0 + sn], True, True)
                act(IU[:, s0:s0 + sn], ps[:, 0:sn], AF.Tanh, bias=HB[:, 8 + u:9 + u], scale=0.5)
            yield
            act(GH[:, 0:n], RA[:, 0:n], AF.Exp, bias=CL[:, 8 + u:9 + u], scale=CL[:, 8 + u:9 + u])
            act(RA[:, 0:n], RA[:, 0:n], AF.Exp, bias=CL[:, u:u + 1], scale=CL[:, u:u + 1])
            stt(IU[:, 0:n], IU[:, 0:n], 1.0, XC[:, 0:n], ALU.add, ALU.mult)
            yield
            act(GH[:, 0:n], GH[:, 0:n], AF.Sqrt, bias=1.0, scale=-1.0)
            yield
            stt(IU[:, 0:n], IU[:, 0:n], 0.5, GH[:, 0:n], ALU.mult, ALU.mult)
            yield
            if d == 0:
                o_, a_, u_ = out_ap, RA[:, 0:n], IU[:, 0:n]
            else:
                o_, a_, u_ = out_ap[:, ::-1], RA[:, n - 1::-1], IU[:, n - 1::-1]
            rd = [RA[:, 0:n], IU[:, 0:n]] + ([] if isinstance(init, float) else [init])
            while dep is not None and dep not in done:
                yield
            P.op("dve", lambda: nc.vector.tensor_tensor_scan(out=o_, data0=a_, data1=u_, initial=init, op0=ALU.mult, op1=ALU.add),
                 reads=rd, writes=[out_ap])
            if post is not None:
                post()
            done.add(uid)
            yield

        def quarters(a, b):
            q = (b - a + 3) // 4
            q = -(-q // 4) * 4
            return blocks(a, b, q)

        units = []
        ui = 0
        for c in range(4):
            for qi, (a, n) in enumerate(quarters(0, NMl)):
                init = 0.0 if qi == 0 else H0[c][:, a - 1:a]
                post = (lambda c=c: act(CO[:, c:c + 1], H0[c][:, CP:CP + 1], AF.Copy)) if qi == 3 else None
                units.append(unit_gen(c, 0, a, a + n, init, H0[c][:, a:a + n], WSETS[ui % NWS], LXQ[ui % NWS],
                                      first=(qi == 0), post=post, uid=(0, c, qi), dep=((0, c, qi - 1) if qi > 0 else None)))
                ui += 1
        run_threads(units, width=NWS)
        P.dma("pool", cc_in.ap(), CO, writes=[("k", "cc_in")])
        P.collective(cc_in, cc_out, [[0, 1], [2, 3], [4, 5], [6, 7]], reads=[("k", "cc_in")], writes=[("k", "cc_out")])
        P.dma("pool", CIN2, cc_out.ap().rearrange("(r p) c -> p r c", r=2), reads=[("k", "cc_out")])
        ts(CIN, CIN2[:, 0, :], pv("mask", 0), 0.0, ALU.mult, ALU.add)
        stt(CIN, CIN2[:, 1, :], pv("mask", 1), CIN, ALU.mult, ALU.add)
        units = []
        for c in range(4):
            qs = quarters(0, D1 + 1)
            for qi in (3, 2, 1, 0):
                a, n = qs[qi]
                ws = WSETS[ui % NWS]
                init = CIN[:, c:c + 1] if qi == 3 else C1[:, c * 4 + qi + 1:c * 4 + qi + 2]

                def post(c=c, qi=qi, a=a, n=n, ws=ws):
                    GHq = ws[3]
                    act(C1[:, c * 4 + qi:c * 4 + qi + 1], GHq[:, 0:1], AF.Copy)
                    tt(H0[c][:, a:a + n], H0[c][:, a:a + n], GHq[:, 0:n], ALU.add)
                    tt(H0[c][:, a:a + n], H0[c][:, a:a + n], G2[c % 2][:, a:a + n], ALU.mult)
                units.append(unit_gen(c, 1, a, a + n, init, ws[3][:, 0:n], ws, LXQ[ui % NWS],
                                      first=(qi == 3), post=post, uid=(1, c, qi), dep=((1, c, qi + 1) if qi < 3 else None)))
                ui += 1
        run_threads(units, width=NWS)
        wload(bf16v(wob0, 2048), wout_d[l, :, 2048:4096])
        wload(bf16v(wob0 + 1024, 2048), wout_d[l, :, 4096:6144])

        def gen_tail(t0, n, sc):
            ps = bank()
            for c in range(4):
                act(SQB[sc][:, c % 2, 0:n], H0[c][:, t0:t0 + n], AF.Square)
                mm(ps[:, 0:n], ONESB, SQB[sc][:, c % 2, 0:n], c == 0, c == 3)
            rsqrt(RSB[sc][:, 0:n], ps[:, 0:n], 1.0 / 512)
            yield
            for c in range(4):
                stt(YB[c][:, t0:t0 + n], H0[c][:, t0:t0 + n], pv("g_out", 2 + c), RSB[sc][:, 0:n], ALU.mult, ALU.mult)
            yield
            for m in range(8):
                ps = bank()
                for k in range(4):
                    mm(ps[:, 0:n], WO_B[:, k, m * 128:(m + 1) * 128], YB[k][:, t0:t0 + n], k == 0, k == 3)
                tt(X[:, m, t0:t0 + n], X[:, m, t0:t0 + n], ps[:, 0:n], ALU.add)
                if m % 2 == 1:
                    yield
            for k in range(8):
                act(SQT[sc][:, k, 0:n], X[:, k, t0:t0 + n], AF.Square)
            ps = bank()
            for k in range(8):
                mm(ps[:, 0:n], ONESB, SQT[sc][:, k, 0:n], k == 0, k == 7)
            rsqrt(RST[sc][:, 0:n], ps[:, 0:n], 1.0 / D)
            yield
            for k in range(8):
                stt(H[:, k, 1 + t0:1 + t0 + n], X[:, k, t0:t0 + n], pv("g_ffn", k), RST[sc][:, 0:n], ALU.mult, ALU.mult)
            yield

        run_threads([gen_tail(t0, n, bi % 2) for bi, (t0, n) in enumerate(blocks(0, NMl))], width=2)

        state["jr"] = 0.0
        o = C_OVL
        AFN = bf16v(o, 22 * 1152).rearrange("p (j t) -> p j t", j=22)
        o += 22 * 576
        FG = f32v(o, 1160)
        FU = f32v(o + 1160, 1160)
        o += 2320
        TG = f32v(o, 1152)
        TU = f32v(o + 1152, 1152)
        o += 2304
        WUP = [[bf16v(o + (s * 2 + g) * 512, 1024) for g in range(2)] for s in range(2)]
        o += 2048
        WDN = [bf16v(o + s * 1408, 2816) for s in range(2)]
        WDNh = [[bf16v(o + s * 1408 + hh * 704, 1408) for hh in range(2)] for s in range(2)]
        o += 2816
        assert o <= ARENA, o
        for hf, (a, b) in enumerate([(0, NOl // 2), (NOl // 2, NOl)]):
            n = b - a
            for j in range(22):
                s = j % 2
                wload(WUP[s][0], wup_d[l, j])
                wload(WUP[s][1], wup_d[l, 22 + j])
                for g in range(2):
                    F = FG if g == 0 else FU
                    T = TG if g == 0 else TU
                    Wt = WUP[s][g].rearrange("p (k m) -> p k m", k=8)
                    for (s0, sn) in blocks(0, n + 2):
                        ps = bank()
                        for k in range(8):
                            mm(ps[:, 0:sn], Wt[:, k, :], H[:, k, a + s0:a + s0 + sn], k == 0, k == 7)
                        act(F[:, s0:s0 + sn], ps[:, 0:sn], AF.Copy)
                    jj = j + 22 * g
                    act(T[:, 0:n], F[:, 0:n], AF.Identity, bias=pv("f_cb", jj), scale=pv("f_cw", jj * 3))
                    stt(T[:, 0:n], F[:, 1:n + 1], pv("f_cw", jj * 3 + 1), T[:, 0:n], ALU.mult, ALU.add)
                    stt(T[:, 0:n], F[:, 2:n + 2], pv("f_cw", jj * 3 + 2), T[:, 0:n], ALU.mult, ALU.add)
                act(TG[:, 0:n], TG[:, 0:n], AF.Gelu_apprx_tanh)
                tt(AFN[:, j, 0:n], TG[:, 0:n], TU[:, 0:n], ALU.mult)
            for m in range(8):
                s = m % 2
                wload(WDNh[s][0], wdn_d[l, m, :, 0:1408])
                wload(WDNh[s][1], wdn_d[l, m, :, 1408:2816])
                Wd = WDN[s].rearrange("p (j m) -> p j m", j=22)
                for (s0, sn) in blocks(0, n):
                    ps = bank()
                    for kj in range(22):
                        mm(ps[:, 0:sn], Wd[:, kj, :], AFN[:, kj, s0:s0 + sn], kj == 0, kj == 21)
                    tt(X[:, m, a + s0:a + s0 + sn], X[:, m, a + s0:a + s0 + sn], ps[:, 0:sn], ALU.add)

        norm_H("g_ple", 0, NOl)
        o = C_OVL
        WG = [bf16v(o + m * 512, 1024).rearrange("p (k m) -> p k m", k=8) for m in range(8)]
        for m in range(8):
            wload(bf16v(o + m * 512, 1024), wg_d[l, m])
        o += 4096
        WPLE = bf16v(o, 2048).rearrange("p (k n) -> p k n", k=2)
        wload(bf16v(o, 2048), wple_d[l])
        o += 1024
        PT = bf16v(o, 1024).rearrange("p (k t) -> p k t", k=2)
        o += 512
        E = f32v(o, 4096).rearrange("p (m t) -> p m t", m=8)
        o += 4096
        GTE = f32v(o, 4096).rearrange("p (m t) -> p m t", m=8)
        o += 4096
        SQE = bf16v(o, 4096).rearrange("p (m t) -> p m t", m=8)
        o += 2048
        RSE = f32v(o, 512)
        TMP = f32v(o + 512, 512)
        o += 1024
        for (t0, n) in blocks(0, NOl):
            P.dma("pool", PT[:, :, 0:n], p_d[l, :, :, t0:t0 + n].rearrange("k p t -> p k t"))
            for m in range(8):
                ps = bank()
                for ko in range(2):
                    mm(ps[:, 0:n], WPLE[:, ko, m * 128:(m + 1) * 128], PT[:, ko, 0:n], ko == 0, ko == 1)
                act(E[:, m, 0:n], ps[:, 0:n], AF.Copy)
                act(SQE[:, m, 0:n], ps[:, 0:n], AF.Square)
            ps = bank()
            for m in range(8):
                mm(ps[:, 0:n], ONESB, SQE[:, m, 0:n], m == 0, m == 7)
            rsqrt(RSE[:, 0:n], ps[:, 0:n], 1.0 / D)
            for m in range(8):
                ps = bank()
                for k in range(8):
                    mm(ps[:, 0:n], WG[m][:, k, :], H[:, k, 1 + t0:1 + t0 + n], k == 0, k == 7)
                act(GTE[:, m, 0:n], ps[:, 0:n], AF.Sigmoid, bias=pv("b_gate", m))
                stt(TMP[:, 0:n], E[:, m, 0:n], pv("g_post", m), RSE[:, 0:n], ALU.mult, ALU.mult)
                tt(TMP[:, 0:n], TMP[:, 0:n], GTE[:, m, 0:n], ALU.mult)
                tt(X[:, m, t0:t0 + n], X[:, m, t0:t0 + n], TMP[:, 0:n], ALU.add)

    o = C_OVL
    OUTB = [f32v(o + i * 4096, 4096).rearrange("p (k t) -> p k t", k=8) for i in range(2)]
    SQ = bf16v(NSQ, 8 * 512).rearrange("p (k t) -> p k t", k=8)
    RS = f32v(NSQ + 2048, 512)
    finals = []
    for bi, (t0, n) in enumerate(blocks(0, OWN)):
        OB = OUTB[bi % 2]
        for k in range(8):
            act(SQ[:, k, 0:n], X[:, k, t0:t0 + n], AF.Square)
        ps = bank()
        for k in range(8):
            mm(ps[:, 0:n], ONESB, SQ[:, k, 0:n], k == 0, k == 7)
        rsqrt(RS[:, 0:n], ps[:, 0:n], 1.0 / D)
        for k in range(8):
            stt(OB[:, k, 0:n], X[:, k, t0:t0 + n], pv("g_final", k), RS[:, 0:n], ALU.mult, ALU.mult)
        finals.append(P.dma("sp", y_d[:, :, t0:t0 + n].rearrange("k p t -> p k t"), OB[:, :, 0:n]))
    P.emit(finals)
    st.close()
    return nc


_CACHE = {}


def _prep_inputs(inp):
    f = lambda a: np.ascontiguousarray(a, dtype=np.float32)
    x = f(inp["x"])
    p = f(inp["p"])
    shared = {}
    w_in = f(inp["w_in"])
    shared["win_t"] = f(w_in.reshape(2, 8, 128, 16, 128).transpose(0, 3, 2, 1, 4).reshape(2, 16, 128, 1024))
    shared["wuv_t"] = f(w_in[:, :, 1536:2048].reshape(2, 8, 128, 512).transpose(0, 2, 1, 3).reshape(2, 128, 4096))
    shared["wout_t"] = f(inp["w_out"].reshape(2, 8, 128, 1024).transpose(0, 2, 1, 3).reshape(2, 128, 8192))
    shared["wup_t"] = f(inp["w_up"].reshape(2, 8, 128, 44, 128).transpose(0, 3, 2, 1, 4).reshape(2, 44, 128, 1024))
    shared["wdn_t"] = f(inp["w_down"].reshape(2, 22, 128, 8, 128).transpose(0, 3, 2, 1, 4).reshape(2, 8, 128, 2816))
    shared["wg_t"] = f(inp["w_ple_gate"].reshape(2, 8, 128, 8, 128).transpose(0, 3, 2, 1, 4).reshape(2, 8, 128, 1024))
    shared["wple_t"] = f(inp["w_ple"].reshape(2, 2, 128, 1024).transpose(0, 2, 1, 3).reshape(2, 128, 2048))
    shared["ident"] = np.eye(128, dtype=np.float32)

    def col8(v):
        return v.reshape(-1, 128).T

    maps = []
    for c in range(NCORES):
        b, side = c // 2, c % 2
        xs = x[b] if side == 0 else x[b, ::-1]
        m = dict(shared)
        m["x_fm"] = f(xs[:WMAX].T.reshape(8, 128, WMAX))
        ps_ = p[:, b] if side == 0 else p[:, b, ::-1]
        m["p_fm"] = f(ps_[:, :WMAX].transpose(0, 2, 1).reshape(2, 2, 128, WMAX))
        pvv = np.zeros((2, 128, 512), np.float32)
        tmb = np.zeros((2, 128, 1024), np.float32)
        gates = np.zeros((2, 128, 16, 128), np.float32)
        wst = np.zeros((2, 128, 4, 128), np.float32)
        for l in range(2):
            def put(name, arr):
                arr = np.asarray(arr, np.float32)
                pvv[l, :, PVC[name]:PVC[name] + arr.shape[1]] = arr
            put("g_mix", col8(inp["norm_mix"][l]))
            put("g_ffn", col8(inp["norm_ffn"][l]))
            put("g_ple", col8(inp["norm_ple"][l]))
            put("g_post", col8(inp["ple_post_norm"][l]))
            put("g_out", col8(inp["out_norm"][l]))
            put("b_gate", col8(inp["b_ple_gate"][l]))
            put("g_final", col8(inp["final_norm"]))
            put("g_next", col8(inp["norm_mix"][1]) if l == 0 else col8(inp["final_norm"]))
            cw = np.asarray(inp["conv_dw_w"][l])
            if side == 1:
                cw = cw[::-1]
            put("conv_w", np.concatenate([cw[:, cc * 128:(cc + 1) * 128].T for cc in range(2)], axis=1))
            put("conv_b", col8(inp["conv_dw_b"][l]))
            put("gn_g", col8(inp["conv_gn_g"][l]))
            put("gn_b", col8(inp["conv_gn_b"][l]))
            lcw = np.zeros((128, 32), np.float32)
            for d in range(2):
                rd = d if side == 0 else 1 - d
                for cc in range(4):
                    for k in range(4):
                        lcw[:, (d * 4 + cc) * 4 + k] = inp["lru_conv_w"][l, rd, k, cc * 128:(cc + 1) * 128]
                    for g, nm in enumerate(("lru_wa", "lru_wx")):
                        for i in range(2):
                            gates[l, 64 * i:64 * i + 64, (d * 4 + cc) * 2 + g, 64 * i:64 * i + 64] = inp[nm][l, rd, 2 * cc + i]
            put("l_cw", lcw)
            dirs = [0, 1] if side == 0 else [1, 0]
            for nm, key in (("l_cb", "lru_conv_b"), ("l_ba", "lru_ba"), ("l_bx", "lru_bx"), ("l_lam", "lru_lambda")):
                put(nm, np.concatenate([col8(inp[key][l, rd]) for rd in dirs], axis=1))
            fw = np.asarray(inp["ffn_conv_w"][l])
            if side == 1:
                fw = fw[::-1]
            put("f_cw", fw.reshape(3, 44, 128).transpose(2, 1, 0).reshape(128, 132))
            put("f_cb", col8(inp["ffn_conv_b"][l]))
            put("mask", np.broadcast_to(np.array([[1.0, 0.0]] if side == 1 else [[0.0, 1.0]], np.float32), (128, 2)))
            tmb[l, :, 0:256] = inp["sgu_ln_g"][l][None, :]
            tmb[l, :, 256:512] = inp["sgu_ln_b"][l][None, :]
            bs = np.asarray(inp["sgu_bs"][l])
            ws = np.asarray(inp["sgu_ws"][l])
            if side == 1:
                bs = bs[:, ::-1]
                ws = ws[:, ::-1, ::-1]
            tmb[l, :, 512:768] = np.repeat(bs.T, 64, axis=1)
            tmb[l, :, 768:1024] = inp["out_norm"][l][None, 768:1024]
            wst[l] = ws.transpose(2, 0, 1)
        m["pv"] = pvv
        m["tmb"] = tmb
        m["gates_t"] = f(gates.reshape(2, 128, 2048))
        m["wst_t"] = f(wst.reshape(2, 128, 512))
        maps.append(m)
    return maps


def kernel(**inputs):
    if "nc" not in _CACHE:
        _CACHE["nc"] = build_program()
    nc = _CACHE["nc"]
    maps = _prep_inputs(inputs)
    res = run_bass_kernel_spmd(nc, maps, core_ids=list(range(NCORES)))
    out = np.zeros((4, SEQ, D), np.float32)
    for c in range(NCORES):
        b, side = c // 2, c % 2
        y = np.asarray(res.results[c]["y_fm"]).reshape(D, OWN).T
        if side == 0:
            out[b, 0:OWN] = y
        else:
            out[b, OWN:] = y[::-1]
    return out
```
